# Optimizing a Trainium2 kernel written in Bass

```python
import math
import jax, jax.numpy as jnp
from jax import lax
import numpy as np

D_MODEL = 1024
BATCH = 4
SEQ = 8192
DEPTH = 1

N_HEADS = 16
HEAD_DIM = 64
N_KV_GROUPS = 4
HEADS_PER_GROUP = N_HEADS // N_KV_GROUPS
CMP_BLOCK = 32
CMP_STRIDE = 16
CMP_HIDDEN = 256
SEL_BLOCK = 64
N_SELECT = 16
WINDOW = 512
NSA_Q_BLOCK = 64
FORCE = 1e4
CONV_DIM = D_MODEL
CONV_WIDTH = 3
N_MEM = 256
MEM_HEADS = 4
MEM_HEAD_DIM = D_MODEL // MEM_HEADS
REL_BUCKETS = 32
REL_MAX_DIST = 128
D_FF = 2816
EPS = 1e-6
NEG = -1e30

NSA_WIDTH = N_HEADS * HEAD_DIM
KV_WIDTH = N_KV_GROUPS * HEAD_DIM
MEM_WIDTH = MEM_HEADS * MEM_HEAD_DIM
IN_SPLITS = (NSA_WIDTH, 6 * KV_WIDTH, 3 * N_HEADS, 3 * CONV_DIM, MEM_WIDTH, 3 * D_MODEL)
IN_WIDTH = NSA_WIDTH + 6 * KV_WIDTH + 3 * N_HEADS + 3 * CONV_DIM + MEM_WIDTH + 3 * D_MODEL

kernel_name = "hybrid_nsa_shortconv_memory_macaron"


def rmsnorm(x, g):
    xf = x.astype(jnp.float32)
    y = xf * lax.rsqrt(jnp.mean(xf * xf, axis=-1, keepdims=True) + EPS)
    return (y * g.astype(jnp.float32)).astype(x.dtype)


def swiglu_ffn(h, w_in, w_out):
    a, b = jnp.split(h @ w_in, 2, axis=-1)
    return (jax.nn.silu(a) * b) @ w_out


def masked_softmax(s, valid):
    return jax.nn.softmax(jnp.where(valid, s.astype(jnp.float32), NEG), axis=-1)


def rel_bucket(dist):
    n = jnp.maximum(dist, 0)
    max_exact = REL_BUCKETS // 2
    nf = jnp.maximum(n, 1).astype(jnp.float32)
    large = max_exact + (jnp.log(nf / max_exact) / math.log(REL_MAX_DIST / max_exact)
                         * (REL_BUCKETS - max_exact)).astype(jnp.int32)
    large = jnp.minimum(large, REL_BUCKETS - 1)
    return jnp.where(n < max_exact, n, large)


def compress(kv, pe, w1, w2):
    b, s, g, d = kv.shape
    ratio = CMP_BLOCK // CMP_STRIDE
    n_chunks = s // CMP_STRIDE
    chunks = kv.reshape(b, n_chunks, CMP_STRIDE, g, d)
    blocks = jnp.concatenate([chunks[:, r:n_chunks - ratio + 1 + r] for r in range(ratio)], axis=2)
    blocks = blocks + pe[None, None, :, None, :]
    n_cmp = blocks.shape[1]
    flat = blocks.transpose(0, 1, 3, 2, 4).reshape(b, n_cmp, g, CMP_BLOCK * d)
    return jax.nn.silu(flat @ w1) @ w2


def cmp_to_sel_matrix(n_cmp, n_sel):
    cs = jnp.arange(n_cmp)[:, None] * CMP_STRIDE
    ss = jnp.arange(n_sel)[None, :] * SEL_BLOCK
    ov = jnp.maximum(jnp.minimum(cs + CMP_BLOCK, ss + SEL_BLOCK) - jnp.maximum(cs, ss), 0)
    return ov.astype(jnp.float32) / CMP_BLOCK


def nsa_attention(q, k_cmp, v_cmp, k_slc, v_slc, k_win, v_win, gates, rel_bias):
    b, s, h, d = q.shape
    g, r = N_KV_GROUPS, HEADS_PER_GROUP
    n_cmp = k_cmp.shape[1]
    n_sel = s // SEL_BLOCK
    top = min(N_SELECT, n_sel)
    scale = d ** -0.5
    qg = q.reshape(b, s, g, r, d)
    gg = gates.reshape(b, s, g, r, 3)
    bias_grk = rel_bias.reshape(REL_BUCKETS, g, r).transpose(1, 0, 2)
    sel_map = cmp_to_sel_matrix(n_cmp, n_sel)
    cmp_end = jnp.arange(n_cmp) * CMP_STRIDE + CMP_BLOCK - 1
    ks_blocks = k_slc.reshape(b, n_sel, SEL_BLOCK, g, d).transpose(0, 3, 1, 2, 4)
    vs_blocks = v_slc.reshape(b, n_sel, SEL_BLOCK, g, d).transpose(0, 3, 1, 2, 4)
    pad = ((0, 0), (WINDOW, 0), (0, 0), (0, 0))
    kw_pad = jnp.pad(k_win, pad)
    vw_pad = jnp.pad(v_win, pad)
    bi = jnp.arange(b)[:, None, None, None]
    gi = jnp.arange(g)[None, :, None, None]

    def dense_bias(dist):
        qn, kn = dist.shape
        return rel_bias[rel_bucket(dist)].reshape(qn, kn, g, r).transpose(2, 3, 0, 1)

    def block(i):
        s0 = i * NSA_Q_BLOCK
        t = s0 + jnp.arange(NSA_Q_BLOCK)
        qb = lax.dynamic_slice_in_dim(qg, s0, NSA_Q_BLOCK, axis=1)
        gb = lax.dynamic_slice_in_dim(gg, s0, NSA_Q_BLOCK, axis=1)
        valid_c = cmp_end[None, :] <= t[:, None]
        s_c = jnp.einsum('bqgrd,bcgd->bgrqc', qb, k_cmp) * scale + dense_bias(t[:, None] - cmp_end[None, :])
        p_c = masked_softmax(s_c, valid_c) * jnp.any(valid_c, axis=-1)[:, None].astype(jnp.float32)
        o_c = jnp.einsum('bgrqc,bcgd->bqgrd', p_c.astype(v_cmp.dtype), v_cmp)
        imp = jnp.einsum('bgrqc,cn->bgqn', p_c, sel_map)
        blk = jnp.arange(n_sel)[None, :]
        cur = (t // SEL_BLOCK)[:, None]
        valid_b = blk <= cur
        forced = (blk == 0) | (blk == cur) | (blk == cur - 1)
        score = jnp.where(valid_b, imp + jnp.where(forced, FORCE, 0.0), -FORCE)
        _, idx = lax.top_k(score, top)
        ks = ks_blocks[bi, gi, idx].reshape(b, g, NSA_Q_BLOCK, top * SEL_BLOCK, d)
        vs = vs_blocks[bi, gi, idx].reshape(b, g, NSA_Q_BLOCK, top * SEL_BLOCK, d)
        pos = (idx[..., None] * SEL_BLOCK + jnp.arange(SEL_BLOCK)).reshape(b, g, NSA_Q_BLOCK, top * SEL_BLOCK)
        dist_s = t[None, None, :, None] - pos
        bias_s = bias_grk[gi, rel_bucket(dist_s)].transpose(0, 1, 4, 2, 3)
        s_s = jnp.einsum('bqgrd,bgqtd->bgrqt', qb, ks) * scale + bias_s
        p_s = masked_softmax(s_s, (dist_s >= 0)[:, :, None])
        o_s = jnp.einsum('bgrqt,bgqtd->bqgrd', p_s.astype(vs.dtype), vs)
        kw = lax.dynamic_slice_in_dim(kw_pad, s0, NSA_Q_BLOCK + WINDOW, axis=1)
        vw = lax.dynamic_slice_in_dim(vw_pad, s0, NSA_Q_BLOCK + WINDOW, axis=1)
        kp = s0 - WINDOW + jnp.arange(NSA_Q_BLOCK + WINDOW)
        dist_w = t[:, None] - kp[None, :]
        valid_w = (kp[None, :] >= 0) & (dist_w >= 0) & (dist_w < WINDOW)
        s_w = jnp.einsum('bqgrd,bkgd->bgrqk', qb, kw) * scale + dense_bias(dist_w)
        p_w = masked_softmax(s_w, valid_w)
        o_w = jnp.einsum('bgrqk,bkgd->bqgrd', p_w.astype(vw.dtype), vw)
        return gb[..., 0:1] * o_c + gb[..., 1:2] * o_s + gb[..., 2:3] * o_w

    out = lax.map(block, jnp.arange(s // NSA_Q_BLOCK))
    return out.transpose(1, 0, 2, 3, 4, 5).reshape(b, s, h * d)


def short_gated_conv(conv_in, conv_w, conv_b):
    gate_b, gate_c, x_in = jnp.split(conv_in, 3, axis=-1)
    u = gate_c * x_in
    y = lax.conv_general_dilated(u, conv_w[:, None, :], window_strides=(1,),
                                 padding=((CONV_WIDTH - 1, 0),),
                                 dimension_numbers=('NWC', 'WIO', 'NWC'),
                                 feature_group_count=u.shape[-1])
    return gate_b * (y + conv_b)


def memory_attention(q_mem, mem, mem_norm_g, w_mem_kv, q_g, k_g):
    b, s, _ = q_mem.shape
    m = mem.shape[1]
    km, vm = jnp.split(rmsnorm(mem, mem_norm_g) @ w_mem_kv, 2, axis=-1)
    km = rmsnorm(km.reshape(b, m, MEM_HEADS, MEM_HEAD_DIM), k_g)
    vm = vm.reshape(b, m, MEM_HEADS, MEM_HEAD_DIM)
    qm = rmsnorm(q_mem.reshape(b, s, MEM_HEADS, MEM_HEAD_DIM), q_g)
    sm = jnp.einsum('bshd,bmhd->bhsm', qm, km) * MEM_HEAD_DIM ** -0.5
    pm = jax.nn.softmax(sm.astype(jnp.float32), axis=-1).astype(vm.dtype)
    return jnp.einsum('bhsm,bmhd->bshd', pm, vm).reshape(b, s, MEM_WIDTH)


def hybrid_layer(x, mem, ffn1_norm_g, ffn1_w_in, ffn1_w_out, mix_norm_g, w_in, q_norm_g, k_norm_g,
                 cmp_pe_k, cmp_w1_k, cmp_w2_k, cmp_pe_v, cmp_w1_v, cmp_w2_v, conv_w, conv_b,
                 mem_norm_g, w_mem_kv, mem_q_norm_g, mem_k_norm_g, w_out,
                 ffn2_norm_g, ffn2_w_in, ffn2_w_out, rel_bias):
    b, s, _ = x.shape
    x = x + 0.5 * swiglu_ffn(rmsnorm(x, ffn1_norm_g), ffn1_w_in, ffn1_w_out)
    h = rmsnorm(x, mix_norm_g)
    split_points = np.cumsum(IN_SPLITS)[:-1].tolist()
    q, kv, nsa_g, conv_in, q_mem, merge_g = jnp.split(h @ w_in, split_points, axis=-1)
    q = rmsnorm(q.reshape(b, s, N_HEADS, HEAD_DIM), q_norm_g)
    kc, vc, ks, vs, kw, vw = [t.reshape(b, s, N_KV_GROUPS, HEAD_DIM) for t in jnp.split(kv, 6, axis=-1)]
    k_cmp = rmsnorm(compress(kc, cmp_pe_k, cmp_w1_k, cmp_w2_k), k_norm_g)
    v_cmp = compress(vc, cmp_pe_v, cmp_w1_v, cmp_w2_v)
    gates = jax.nn.sigmoid(nsa_g.astype(jnp.float32)).astype(x.dtype).reshape(b, s, N_HEADS, 3)
    o_nsa = nsa_attention(q, k_cmp, v_cmp, rmsnorm(ks, k_norm_g), vs, rmsnorm(kw, k_norm_g), vw,
                          gates, rel_bias)
    o_conv = short_gated_conv(conv_in, conv_w, conv_b)
    o_mem = memory_attention(q_mem, mem, mem_norm_g, w_mem_kv, mem_q_norm_g, mem_k_norm_g)
    g_nsa, g_conv, g_mem = jnp.split(jax.nn.sigmoid(merge_g.astype(jnp.float32)).astype(x.dtype), 3, axis=-1)
    x = x + (g_nsa * o_nsa + g_conv * o_conv + g_mem * o_mem) @ w_out
    x = x + 0.5 * swiglu_ffn(rmsnorm(x, ffn2_norm_g), ffn2_w_in, ffn2_w_out)
    return x


def setup_inputs(seed: int = 0) -> dict:
    key = jax.random.key(seed)
    keys = iter(jax.random.split(key, 32))
    L = DEPTH

    def nrm(shape, scale):
        return jax.random.normal(next(keys), shape, jnp.float32) * scale

    def w(shape, fan_in):
        return nrm(shape, fan_in ** -0.5)

    def gain(shape):
        return 1.0 + nrm(shape, 0.05)

    return {
        "x": nrm((BATCH, SEQ, D_MODEL), 1.0),
        "mem": nrm((BATCH, N_MEM, D_MODEL), 1.0),
        "ffn1_norm_g": gain((L, D_MODEL)),
        "ffn1_w_in": w((L, D_MODEL, 2 * D_FF), D_MODEL),
        "ffn1_w_out": w((L, D_FF, D_MODEL), D_FF),
        "mix_norm_g": gain((L, D_MODEL)),
        "w_in": w((L, D_MODEL, IN_WIDTH), D_MODEL),
        "q_norm_g": gain((L, HEAD_DIM)),
        "k_norm_g": gain((L, HEAD_DIM)),
        "cmp_pe_k": nrm((L, CMP_BLOCK, HEAD_DIM), 0.1),
        "cmp_w1_k": w((L, CMP_BLOCK * HEAD_DIM, CMP_HIDDEN), CMP_BLOCK * HEAD_DIM),
        "cmp_w2_k": w((L, CMP_HIDDEN, HEAD_DIM), CMP_HIDDEN),
        "cmp_pe_v": nrm((L, CMP_BLOCK, HEAD_DIM), 0.1),
        "cmp_w1_v": w((L, CMP_BLOCK * HEAD_DIM, CMP_HIDDEN), CMP_BLOCK * HEAD_DIM),
        "cmp_w2_v": w((L, CMP_HIDDEN, HEAD_DIM), CMP_HIDDEN),
        "conv_w": w((L, CONV_WIDTH, CONV_DIM), CONV_WIDTH),
        "conv_b": nrm((L, CONV_DIM), 0.02),
        "mem_norm_g": gain((L, D_MODEL)),
        "w_mem_kv": w((L, D_MODEL, 2 * MEM_WIDTH), D_MODEL),
        "mem_q_norm_g": gain((L, MEM_HEAD_DIM)),
        "mem_k_norm_g": gain((L, MEM_HEAD_DIM)),
        "w_out": w((L, D_MODEL, D_MODEL), D_MODEL),
        "ffn2_norm_g": gain((L, D_MODEL)),
        "ffn2_w_in": w((L, D_MODEL, 2 * D_FF), D_MODEL),
        "ffn2_w_out": w((L, D_FF, D_MODEL), D_FF),
        "rel_bias": nrm((REL_BUCKETS, N_HEADS), 0.5),
    }


def reference(x, mem, ffn1_norm_g, ffn1_w_in, ffn1_w_out, mix_norm_g, w_in, q_norm_g, k_norm_g,
              cmp_pe_k, cmp_w1_k, cmp_w2_k, cmp_pe_v, cmp_w1_v, cmp_w2_v, conv_w, conv_b,
              mem_norm_g, w_mem_kv, mem_q_norm_g, mem_k_norm_g, w_out,
              ffn2_norm_g, ffn2_w_in, ffn2_w_out, rel_bias):
    for l in range(DEPTH):
        x = hybrid_layer(x, mem, ffn1_norm_g[l], ffn1_w_in[l], ffn1_w_out[l], mix_norm_g[l], w_in[l],
                         q_norm_g[l], k_norm_g[l], cmp_pe_k[l], cmp_w1_k[l], cmp_w2_k[l],
                         cmp_pe_v[l], cmp_w1_v[l], cmp_w2_v[l], conv_w[l], conv_b[l],
                         mem_norm_g[l], w_mem_kv[l], mem_q_norm_g[l], mem_k_norm_g[l], w_out[l],
                         ffn2_norm_g[l], ffn2_w_in[l], ffn2_w_out[l], rel_bias)
    return x
```

```python
import math
from contextlib import ExitStack
from functools import partial
import numpy as np
import concourse.bass as bass
import concourse.mybir as mybir
from concourse.bass_utils import run_bass_kernel_spmd

F32 = mybir.dt.float32
BF16 = mybir.dt.bfloat16
ALU = mybir.AluOpType
AF = mybir.ActivationFunctionType
AX = mybir.AxisListType

D = 1024; SEQ = 8192; DFF = 2816; NT = 16; TL = 512; OW = 256
NEGM = -30000.0
FORCE = 1.0e4
EPS = 1e-6
import os
DEBUG = bool(int(os.environ.get('KDEBUG', '0')))
NTILES = int(os.environ.get('KNTILES', str(NT)))
STAGE = float(os.environ.get('KSTAGE', '99'))


class StopBuild(Exception):
    pass


def stage(n):
    if STAGE < n:
        raise StopBuild()


class Res:
    __slots__ = ("w", "r", "excl")

    def __init__(self, excl=False):
        self.w = None
        self.r = {}
        self.excl = excl


class Sem:
    def __init__(self, h):
        self.h = h
        self.n = 0


class Sched:
    def __init__(self, nc, st):
        self.nc = nc
        self.st = st
        self.E = {}
        for name in ("pe", "act", "dve", "pool", "sp"):
            self.E[name] = dict(sem=Sem(st.enter_context(nc.semaphore("sem_" + name))), ops=[], waited={})
        self.nsem = 0

    def new_sem(self):
        self.nsem += 1
        return Sem(self.st.enter_context(self.nc.semaphore("dsem%d" % self.nsem)))

    def op(self, eng, fn, reads=(), writes=(), dsem=None):
        E = self.E[eng]
        need = {}

        def add(tok):
            if tok is None:
                return
            k, v = tok
            if need.get(k, 0) < v:
                need[k] = v
        for r in reads:
            add(r.w)
            if r.excl:
                for k, v in r.r.items():
                    if k is not E["sem"]:
                        add((k, v))
        for w in writes:
            add(w.w)
            for k, v in w.r.items():
                add((k, v))
        if dsem is None:
            sem = E["sem"]; sem.n += 1; inc = 1
        else:
            sem = dsem; sem.n += 16; inc = 16
        tok = (sem, sem.n)
        waits = []
        for k, v in need.items():
            if eng == "pe" and k is E["sem"]:
                continue
            if E["waited"].get(k, 0) < v:
                E["waited"][k] = v
                waits.append((k, v))
        E["ops"].append((waits, fn, sem, inc))
        for r in reads:
            if r.r.get(sem, 0) < tok[1]:
                r.r[sem] = tok[1]
        for w in writes:
            w.w = tok
            w.r = {}
        return tok

    def emit(self, final_waits):
        nc = self.nc
        with nc.Block() as blk:
            def runner(name):
                ops = self.E[name]["ops"]

                def f(e):
                    for waits, fn, sem, inc in ops:
                        for k, v in waits:
                            e.wait_ge(k.h, v)
                        fn(e).then_inc(sem.h, inc)
                    if name == "sp":
                        for k, v in final_waits:
                            e.wait_ge(k.h, v)
                return f
            blk.tensor(runner("pe"))
            blk.scalar(runner("act"))
            blk.vector(runner("dve"))
            blk.gpsimd(runner("pool"))
            blk.sync(runner("sp"))


def alias(olds, news):
    m = {}
    for o in olds:
        if o.w is not None and m.get(o.w[0], 0) < o.w[1]:
            m[o.w[0]] = o.w[1]
        for k, v in o.r.items():
            if m.get(k, 0) < v:
                m[k] = v
    for n in news:
        n.w = None
        n.r = dict(m)


def rel_bucket_np(dist):
    n = np.maximum(dist, 0)
    nf = np.maximum(n, 1).astype(np.float32)
    large = 16 + (np.log(nf / np.float32(16)) / np.float32(math.log(8.0)) * np.float32(16)).astype(np.int32)
    large = np.minimum(large, 31)
    return np.where(n < 16, n, large)


def ws_layout(W, G):
    K, N = W.shape
    KC = K // 128
    nch = N // 128
    assert nch % G == 0
    A = W.reshape(KC, 128, nch // G, G, 128).transpose(2, 1, 3, 0, 4)
    return np.ascontiguousarray(A).reshape(nch // G, 128, G * KC * 128)


def tm_layout(W):
    K, N = W.shape
    KC = K // 128
    return np.ascontiguousarray(W.reshape(KC, 128, N).transpose(1, 0, 2)).reshape(128, KC * N)


def col_layout(v):
    return np.ascontiguousarray(v.reshape(-1, 128).T)


class Builder:
    def __init__(self):
        self.nc = bass.Bass("TRN2", target_bir_lowering=False)
        self.st = ExitStack()
        self.S = Sched(self.nc, self.st)
        self.din = {}
        self.dres = {}

    def dram_in(self, name, shape, dtype=F32):
        ap = self.nc.dram_tensor(name, list(shape), dtype, kind="ExternalInput").ap()
        self.din[name] = ap
        return ap

    def sb(self, name, shape, dtype):
        return self.st.enter_context(self.nc.sbuf_tensor("sb_" + name, list(shape), dtype))

    def ps(self, name, shape, dtype=F32):
        return self.st.enter_context(self.nc.psum_tensor(name, list(shape), dtype))


def build_program():
    B = Builder()
    nc, S, st = B.nc, B.S, B.st
    op = S.op

    xT_d = B.dram_in("xT", [D, SEQ])
    memT_d = B.dram_in("memT", [D, 256])
    avec_d = B.dram_in("avec", [128, 2])
    w1in_d = B.dram_in("w1in", [11, 128, 4 * 8 * 128])
    w1out_d = B.dram_in("w1out", [8, 128, 22 * 128])
    w2in_d = B.dram_in("w2in", [11, 128, 4 * 8 * 128])
    w2out_d = B.dram_in("w2out", [8, 128, 22 * 128])
    wkc_d = B.dram_in("wkc", [2, 128, 4 * 8 * 128])
    wkn_d = B.dram_in("wkn", [1, 128, 4 * 8 * 128])
    wv_d = B.dram_in("wv", [1, 128, 8 * 512])
    wconv_d = B.dram_in("wconv", [8, 128, 3 * 8 * 128])
    wq_d = B.dram_in("wq", [2, 128, 4 * 8 * 128])
    wqm_d = B.dram_in("wqm", [2, 128, 4 * 8 * 128])
    wmg_d = B.dram_in("wmg", [6, 128, 4 * 8 * 128])
    wg_d = B.dram_in("wg", [128, 8 * 48])
    wo_d = B.dram_in("wo", [2, 128, 4 * 8 * 128])
    wmk_d = B.dram_in("wmk", [2, 128, 4 * 8 * 128])
    wmv_d = B.dram_in("wmv", [2, 128, 8 * 512])
    cw1_d = B.dram_in("cw1", [2, 128, 16 * 256])
    cw2_d = B.dram_in("cw2", [128, 2 * 2 * 128 + 2 * 64])
    cpe_d = B.dram_in("cpe", [128, 2 * 16])
    gains_d = B.dram_in("gains", [128, 64])
    GS_d = B.dram_in("GS", [3, 128, 2048])
    R31_d = B.dram_in("R31", [128, 2048])
    MS_d = B.dram_in("MS", [3, 128, 2048])
    MW_d = B.dram_in("MW", [2, 128, 512])
    GC_d = B.dram_in("GC", [2, 64, 2048])
    MC_d = B.dram_in("MC", [2, 64, 2048])
    addtab_d = B.dram_in("addtab", [128, 256])
    selmap_d = B.dram_in("selmap", [128, 5 * 128])
    E32_d = B.dram_in("E32", [32, 2048])
    Iext_d = B.dram_in("Iext", [64, 320])
    cmat_d = B.dram_in("cmat", [128, 3 * 128])
    ones65_d = B.dram_in("ones65", [128, 5 * 4])
    outT_d = nc.dram_tensor("outT", [D, SEQ // 2], F32, kind="ExternalOutput").ap()
    kslc_d = nc.dram_tensor("kslc_s", [2, 128, SEQ], BF16, kind="Internal").ap()
    vslc_d = nc.dram_tensor("vslc_s", [4, 128, 64 * 65], BF16, kind="Internal").ap()
    dbg = {}
    if DEBUG:
        dbg["x1"] = nc.dram_tensor("dbg_x1", [128, 8 * 512], F32, kind="ExternalOutput").ap()
        dbg["onsa"] = nc.dram_tensor("dbg_onsa", [2, 128, 1024], F32, kind="ExternalOutput").ap()
        dbg["z"] = nc.dram_tensor("dbg_z", [128, 8 * 256], F32, kind="ExternalOutput").ap()
        dbg["imp"] = nc.dram_tensor("dbg_imp", [8, 128, 128], F32, kind="ExternalOutput").ap()

    avec = B.sb("avec", [128, 2], F32)
    gains = B.sb("gains", [128, 64], F32)
    cmat = B.sb("cmat", [128, 384], BF16)
    ident = cmat[:, 0:128]; BDm = cmat[:, 128:256]; ones = cmat[:, 256:384]
    E32 = B.sb("E32", [32, 2048], BF16)
    Iext = B.sb("Iext", [64, 320], BF16)
    biasS = B.sb("biasS", [128, 3 * 2048], BF16)
    maskW = B.sb("maskW", [128, 2 * 512], BF16)
    biasC = B.sb("biasC", [64, 2 * 2048], BF16)
    addtab = B.sb("addtab", [128, 256], F32)
    x1own = B.sb("x1own", [128, 8 * OW], F32)
    hown = B.sb("hown", [128, 8 * OW], BF16)
    convown = B.sb("convown", [128, 8 * OW], BF16)
    kwinT = B.sb("kwinT", [128, 2 * 1024], BF16)
    vwin = B.sb("vwin", [128, 8 * 4 * 65], BF16)
    kcmpT = B.sb("kcmpT", [128, 2 * 576], BF16)
    cmpV = B.sb("cmpV", [128, 5 * 4 * 193], BF16)
    kcT2 = B.sb("kcT2", [128, 2 * 4 * 528], BF16)
    ubuf = B.sb("ubuf", [128, 8 * 514], BF16)
    kmT = B.sb("kmT", [128, 8 * 256], BF16)
    vm = B.sb("vm", [128, 2 * 1024], BF16)
    wgate = B.sb("wgate", [128, 8 * 48], BF16)
    cw2 = B.sb("cw2", [128, 640], BF16)
    cpe = B.sb("cpe", [128, 32], BF16)
    cbias = B.sb("cbias", [128, 4], F32)
    hidVpad = B.sb("hidVpad", [128, 2 * 4 * 128], BF16)
    wslots = [B.sb("wslot%d" % i, [128, 4096], BF16) for i in range(2)]
    kring = [B.sb("kring%d" % i, [128, 1024], BF16) for i in range(2)]
    vring = [B.sb("vring%d" % i, [128, 8 * 65], BF16) for i in range(2)]
    ovF = B.sb("ovF", [128, 4096], F32)
    ovB = B.sb("ovB", [128, 26624], BF16)
    small = B.sb("small", [128, 512], F32)
    tmpF = [B.sb("tmpF%d" % i, [128, 512], F32) for i in range(4)]
    stageF = ovF[:, 0:2048]

    PB = [B.ps("pb%d" % i, [128, 512], F32) for i in range(7)]
    PT = B.ps("pt", [128, 1024], BF16)
    PBr = [Res(True) for _ in range(7)]
    PTr = Res(True)

    xT = ovF
    hT = ovB[:, 0:4096]
    actT = ovB[:, 4096:4096 + 11264]
    sqT = ovB[:, 4096:4096 + 4096]
    convall = ovB[:, 15360:15360 + 4096]
    ktile = ovB[:, 19456:19456 + 1024]
    vtile = ovB[:, 20480:20480 + 1040]
    hidT = ovB[:, 21520:21520 + 256]
    sqs = ovB[:, 21776:21776 + 512]
    onsa = ovF
    o2 = 0
    qT = ovB[:, o2:o2 + 2048]; o2 += 2048
    qmT = ovB[:, o2:o2 + 2048]; o2 += 2048
    sg = ovB[:, o2:o2 + 6144]; o2 += 6144
    zT = ovB[:, o2:o2 + 2048]; o2 += 2048
    onsaT = ovB[:, o2:o2 + 2048]; o2 += 2048
    omemT = ovB[:, o2:o2 + 2048]; o2 += 2048
    h2 = qT
    act2 = sg[:, 0:5632]
    sq2 = qmT
    sqq = ovB[:, o2:o2 + 512]; o2 += 512
    pTs = [ovB[:, o2 + i * 512:o2 + (i + 1) * 512] for i in range(3)]; o2 += 1536
    negT4 = ovB[:, o2:o2 + 2048]; o2 += 2048
    onsab = ovB[:, o2:o2 + 1024]; o2 += 1024
    negm = ovB[:, o2:o2 + 128]; o2 += 128
    pmT = ovB[:, o2:o2 + 512]; o2 += 512
    qz = [ovB[:, o2 + i * 2048:o2 + (i + 1) * 2048] for i in range(2)]; o2 += 4096
    assert o2 <= 26624, o2

    R = {}

    def res(name):
        if name not in R:
            R[name] = Res()
        return R[name]
    ph1 = [res(n) for n in ("xT", "hT", "actT", "convall", "ktile", "vtile", "hidT", "sqs")]
    ph2 = [res(n) for n in ("onsa0", "onsa1", "qT", "qmT", "sg", "zT", "onsaT", "omemT", "h2", "act2", "sq2",
                            "pT0", "pT1", "pT2", "negT4", "onsab", "negm", "pmT", "sqq", "qz0", "qz1")]

    dsems = {}

    def dsem(name):
        if name not in dsems:
            dsems[name] = S.new_sem()
        return dsems[name]

    def dma(q, out, in_, reads, writes, sem):
        return op(q, lambda e, out=out, in_=in_: e.dma_start(out=out, in_=in_), reads=reads, writes=writes, dsem=dsem(sem))

    rot = {"i": 0}

    def next_bank(pool=(0, 1, 2, 3, 4, 5)):
        i = pool[rot["i"] % len(pool)]
        rot["i"] += 1
        return PB[i], PBr[i]

    wrot = {"i": 0}
    wres = [Res(), Res()]

    def next_wslot():
        i = wrot["i"] % 2
        wrot["i"] += 1
        return wslots[i], wres[i], "wsl%d" % i

    dres = Res()

    def load_w(src_ap, nelem):
        slot, r, sname = next_wslot()
        dma("pool", slot[:, 0:nelem], src_ap, [dres], [r], sname)
        return slot, r

    def mm(out, lhsT, rhs, start, stop, reads, writes, skip=False):
        def f(e, out=out, lhsT=lhsT, rhs=rhs, start=start, stop=stop, skip=skip):
            if skip:
                return e.matmul(out, lhsT=lhsT, rhs=rhs, start=start, stop=stop, skip_group_check=True)
            return e.matmul(out, lhsT=lhsT, rhs=rhs, start=start, stop=stop)
        return op("pe", f, reads=reads, writes=writes)

    def act(out, in_, func, reads, writes, bias=None, scale=None):
        def f(e, out=out, in_=in_, func=func, bias=bias, scale=scale):
            kw = {}
            if bias is not None:
                kw["bias"] = bias
            if scale is not None:
                kw["scale"] = scale
            return e.activation(out=out, in_=in_, func=func, **kw)
        return op("act", f, reads=reads, writes=writes)

    def v_tt(eng, out, in0, in1, alu, reads, writes):
        return op(eng, lambda e, out=out, in0=in0, in1=in1, alu=alu: e.tensor_tensor(out=out, in0=in0, in1=in1, op=alu),
                  reads=reads, writes=writes)

    def v_ts(eng, out, in0, s1, s2, op0, op1, reads, writes):
        def f(e, out=out, in0=in0, s1=s1, s2=s2, op0=op0, op1=op1):
            if op1 is None:
                return e.tensor_scalar(out=out, in0=in0, scalar1=s1, scalar2=None, op0=op0)
            return e.tensor_scalar(out=out, in0=in0, scalar1=s1, scalar2=s2, op0=op0, op1=op1)
        return op(eng, f, reads=reads, writes=writes)

    def v_stt(eng, out, in0, scalar, in1, op0, op1, reads, writes):
        return op(eng, lambda e, out=out, in0=in0, scalar=scalar, in1=in1, op0=op0, op1=op1:
                  e.scalar_tensor_tensor(out=out, in0=in0, scalar=scalar, in1=in1, op0=op0, op1=op1),
                  reads=reads, writes=writes)

    def v_copy(eng, out, in_, reads, writes):
        return op(eng, lambda e, out=out, in_=in_: e.tensor_copy(out=out, in_=in_), reads=reads, writes=writes)

    def v_memset(eng, ap, val, writes):
        return op(eng, lambda e, ap=ap, val=val: e.memset(ap, val), reads=(), writes=writes)

    def v3(ap, a, b):
        return ap.rearrange("p (a b) -> p a b", a=a, b=b)

    GC = dict(ffn1=0, mix=8, ffn2=16, memn=24, gq8=32, gk=33, gqm=34, gkm=36, cw0=38, cw1=46, cw2=54, cb=62 - 8)
    gains2_d = B.dram_in("gains2", [128, 32])
    gains2 = B.sb("gains2", [128, 32], F32)

    cres = res("consts")
    stg = res("xT")
    for (dst, src) in ((avec[:], avec_d), (gains[:], gains_d), (gains2[:], gains2_d), (addtab[:], addtab_d)):
        dma("sp", dst, src, [dres], [cres], "cst")
    for (dst, src) in ((cmat[:], cmat_d), (E32[:], E32_d), (Iext[:], Iext_d), (wgate[:], wg_d), (cw2[:], cw2_d), (cpe[:], cpe_d)):
        dma("pool", dst, src, [dres], [cres], "cstp")
    v_ts("dve", gains[:, 32:33], gains[:, 32:33], 0.125, None, ALU.mult, None, [cres], [cres])
    v_ts("dve", gains[:, 34:36], gains[:, 34:36], 1.0 / 16.0, None, ALU.mult, None, [cres], [cres])
    for o in range(3):
        dma("sp", stageF[:, :], GS_d[o], [dres], [stg], "stg")
        dma("sp", tmpF[0][:, :], R31_d[:, 0:512], [dres], [res("tmpF0")], "stg2")
        for q in range(4):
            if q > 0:
                dma("sp", tmpF[0][:, :], R31_d[:, q * 512:(q + 1) * 512], [dres], [res("tmpF0")], "stg2")
            v_tt("dve", stageF[:, q * 512:(q + 1) * 512], stageF[:, q * 512:(q + 1) * 512], tmpF[0][:, :], ALU.subtract,
                 [stg, res("tmpF0")], [stg])
            dma("sp", tmpF[1][:, :], MS_d[o][:, q * 512:(q + 1) * 512], [dres], [res("tmpF1")], "stg3")
            v_tt("dve", biasS[:, o * 2048 + q * 512:o * 2048 + (q + 1) * 512], stageF[:, q * 512:(q + 1) * 512], tmpF[1][:, :],
                 ALU.add, [stg, res("tmpF1")], [cres])
    for o in range(2):
        dma("sp", tmpF[1][:, :], MW_d[o], [dres], [res("tmpF1")], "stg3")
        v_copy("dve", maskW[:, o * 512:(o + 1) * 512], tmpF[1][:, :], [res("tmpF1")], [cres])
    for jp in range(2):
        dma("sp", stageF[0:64, :], GC_d[jp], [dres], [stg], "stg")
        for q in range(4):
            dma("sp", tmpF[0][0:64, :], R31_d[0:64, q * 512:(q + 1) * 512], [dres], [res("tmpF0")], "stg2")
            v_tt("dve", stageF[0:64, q * 512:(q + 1) * 512], stageF[0:64, q * 512:(q + 1) * 512], tmpF[0][0:64, :],
                 ALU.subtract, [stg, res("tmpF0")], [stg])
            dma("sp", tmpF[1][0:64, :], MC_d[jp][:, q * 512:(q + 1) * 512], [dres], [res("tmpF1")], "stg3")
            v_tt("dve", biasC[:, jp * 2048 + q * 512:jp * 2048 + (q + 1) * 512], stageF[0:64, q * 512:(q + 1) * 512],
                 tmpF[1][0:64, :], ALU.add, [stg, res("tmpF1")], [cres])
    v_memset("pool", kwinT[:, :], 0.0, [res("kwinT")])
    v_memset("pool", vwin[:, :], 0.0, [res("vwin")])
    v_memset("pool", kcmpT[:, :], 0.0, [res("kcmpT")])
    v_memset("pool", kcT2[:, :], 0.0, [res("kcT2")])
    v_memset("pool", ubuf[:, :], 0.0, [res("ubuf")])
    v_memset("pool", hidVpad[:, :], 0.0, [res("hidVpad")])
    v_memset("pool", ovB[:, 20480:20480 + 1040], 1.0, [res("vtile")])
    cmpV4 = cmpV.rearrange("p (c g x) -> p c g x", c=5, g=4, x=193)
    v_memset("pool", cmpV[:, :], 0.0, [res("cmpV")])
    vwin4 = vwin.rearrange("p (s g x) -> p s g x", s=8, g=4, x=65)
    v_memset("pool", vwin4[:, :, :, 64:65], 1.0, [res("vwin")])
    dma("sp", stageF[:, 0:640], selmap_d, [dres], [stg], "stg")
    for g in range(4):
        v_copy("dve", cmpV4[:, :, g, 65:193], v3(stageF[:, 0:640], 5, 128), [stg], [res("cmpV")])
    dma("sp", tmpF[2][:, 0:20], ones65_d, [dres], [res("tmpF2")], "stg4")
    v_copy("dve", cmpV4[:, :, :, 64], v3(tmpF[2][:, 0:20], 5, 4), [res("tmpF2")], [res("cmpV")])

    def rmsnorm(src, src_res, gcol, N, dst, dst_res, sq, sq_res, nfeat_inv=1.0 / 1024.0):
        act(sq, src, AF.Square, [src_res], [sq_res])
        pb, pr = next_bank()
        for c in range(8):
            mm(pb[:, 0:N], ones, sq[:, c * N:(c + 1) * N], c == 0, c == 7, [sq_res, cres], [pr])
        rs = tmpF[2]
        act(rs[:, 0:N], pb[:, 0:N], AF.Sqrt, [pr], [res("tmpF2")], bias=EPS, scale=nfeat_inv)
        op("dve", lambda e: e.reciprocal(out=rs[:, 0:N], in_=rs[:, 0:N]), reads=[res("tmpF2")], writes=[res("tmpF2")])
        for c in range(8):
            v_stt("dve", dst[:, c * N:(c + 1) * N], src[:, c * N:(c + 1) * N], gains[:, gcol + c:gcol + c + 1], rs[:, 0:N],
                  ALU.mult, ALU.mult, [src_res, res("tmpF2"), cres], [dst_res])

    def ws_proj(w_d, ngrp, G, KC, rhs_fn, rhs_res, N, evac, pool=(0, 1, 2, 3, 4, 5)):
        for grp in range(ngrp):
            slot, wr = load_w(w_d[grp], G * KC * 128)
            for gi in range(G):
                pb, pr = next_bank(pool)
                for kc in range(KC):
                    base = (gi * KC + kc) * 128
                    mm(pb[:, 0:N], slot[:, base:base + 128], rhs_fn(kc), kc == 0, kc == KC - 1, [wr] + rhs_res, [pr])
                evac(grp * G + gi, pb, pr)

    def ffn(w_in_d, w_out_d, h, h_res, N, a_buf, a_res, xres_ap, xres_res):
        state = {}

        def evac_in(idx, pb, pr):
            f = idx // 2
            if idx % 2 == 0:
                t = tmpF[f % 2]
                act(t[:, 0:N], pb[:, 0:N], AF.Silu, [pr], [res("tmpF%d" % (f % 2))])
                state["t"] = (t, res("tmpF%d" % (f % 2)))
            else:
                t, tr = state["t"]
                v_tt("dve", a_buf[:, f * N:(f + 1) * N], t[:, 0:N], pb[:, 0:N], ALU.mult, [tr, pr], [a_res])
        ws_proj(w_in_d, 11, 4, 8, lambda kc: h[:, kc * N:(kc + 1) * N], [h_res], N, evac_in)

        def evac_out(idx, pb, pr):
            v_stt("dve", xres_ap[:, idx * N:(idx + 1) * N], pb[:, 0:N], 0.5, xres_ap[:, idx * N:(idx + 1) * N],
                  ALU.mult, ALU.add, [pr, xres_res], [xres_res])
        ws_proj(w_out_d, 8, 1, 22, lambda kc: a_buf[:, kc * N:(kc + 1) * N], [a_res], N, evac_out)

    def headnorm(pb, pr, N, gcol_ap, dst, dst_res, nrows=128, inv=1.0 / 64.0, reduce_mat=None):
        sq = sqs
        act(sq[0:nrows, 0:N], pb[0:nrows, 0:N], AF.Square, [pr], [res("sqs")])
        p2, p2r = PB[6], PBr[6]
        mm(p2[0:nrows, 0:N], (BDm if reduce_mat is None else reduce_mat)[0:nrows, 0:nrows], sq[0:nrows, 0:N], True, True,
           [res("sqs"), cres], [p2r])
        rs = tmpF[2]
        act(rs[0:nrows, 0:N], p2[0:nrows, 0:N], AF.Sqrt, [p2r], [res("tmpF2")], bias=EPS, scale=inv)
        op("dve", lambda e: e.reciprocal(out=rs[0:nrows, 0:N], in_=rs[0:nrows, 0:N]), reads=[res("tmpF2")], writes=[res("tmpF2")])
        v_stt("dve", dst, pb[0:nrows, 0:N], gcol_ap, rs[0:nrows, 0:N], ALU.mult, ALU.mult, [pr, res("tmpF2"), cres], [dst_res])

    try:
        stage(1)
        memx = ovF[:, 0:2048]
        dma("sp", v3(memx, 8, 256), memT_d.rearrange("(c p) n -> p c n", p=128), [dres], [res("xT")], "xin")
        memn = ovB[:, 0:2048]
        rmsnorm(memx, res("xT"), 24, 256, memn, res("hT"), ovB[:, 4096:4096 + 2048], res("actT"))
        kraw = [tmpF[0], tmpF[1]]

        def evac_mk(idx, pb, pr):
            i = idx % 2
            v_copy("dve", kraw[i][:, 0:256], pb[:, 0:256], [pr], [res("tmpF%d" % i)])
            act(sq2_m[:, i * 256:(i + 1) * 256], pb[:, 0:256], AF.Square, [pr], [res("sqs")])
            if i == 1:
                hm = idx // 2
                p2, p2r = PB[6], PBr[6]
                for ii in range(2):
                    mm(p2[:, 0:256], ones, sq2_m[:, ii * 256:(ii + 1) * 256], ii == 0, ii == 1, [res("sqs"), cres], [p2r])
                rs = tmpF[2]
                act(rs[:, 0:256], p2[:, 0:256], AF.Sqrt, [p2r], [res("tmpF2")], bias=EPS, scale=1.0 / 256.0)
                op("dve", lambda e: e.reciprocal(out=rs[:, 0:256], in_=rs[:, 0:256]), reads=[res("tmpF2")], writes=[res("tmpF2")])
                for ii in range(2):
                    c = hm * 2 + ii
                    v_stt("dve", kmT[:, c * 256:(c + 1) * 256], kraw[ii][:, 0:256], gains[:, 36 + ii:37 + ii], rs[:, 0:256],
                          ALU.mult, ALU.mult, [res("tmpF%d" % ii), res("tmpF2"), cres], [res("kmT")])
        sq2_m = sqs
        ws_proj(wmk_d, 2, 4, 8, lambda kc: memn[:, kc * 256:(kc + 1) * 256], [res("hT")], 256, evac_mk)
        for half in range(2):
            slot, wr = load_w(wmv_d[half], 4096)
            for mc in range(2):
                pb, pr = next_bank()
                for kc in range(8):
                    mm(pb[:, 0:512], memn[:, kc * 256 + mc * 128:kc * 256 + mc * 128 + 128], slot[:, kc * 512:(kc + 1) * 512],
                       kc == 0, kc == 7, [wr, res("hT")], [pr])
                act(vm[:, mc * 1024 + half * 512:mc * 1024 + half * 512 + 512], pb[:, 0:512], AF.Copy, [pr], [res("vm")])
        for typ in range(2):
            slot, wr = load_w(cw1_d[typ], 4096)
            pb, pr = PB[6], PBr[6]
            for hc in range(2):
                for jlo in range(16):
                    mm(pb[:, hc:hc + 1], slot[:, jlo * 256 + hc * 128:jlo * 256 + hc * 128 + 128],
                       cpe[:, typ * 16 + jlo:typ * 16 + jlo + 1], jlo == 0, jlo == 15, [wr, cres], [pr])
            v_copy("dve", cbias[:, typ * 2:typ * 2 + 2], pb[:, 0:2], [pr], [cres])

        stage(2)
        out_tokens = []
        for T in range(NTILES):
            alias(ph2 + ph1, ph1)
            xr = res("xT")
            dma("sp", v3(xT[:, :], 8, 512), xT_d.rearrange("(c p) n -> p c n", p=128)[:, :, T * TL:(T + 1) * TL], [dres], [xr], "xin")
            rmsnorm(xT[:, :], xr, 0, 512, hT, res("hT"), sqT, res("actT"))
            ffn(w1in_d, w1out_d, hT, res("hT"), 512, actT, res("actT"), xT, xr)
            if DEBUG and T == NTILES - 1:
                dma("sp", dbg["x1"], xT[:, :], [xr], [res("dbgd")], "dbg")
            rmsnorm(xT[:, :], xr, 8, 512, hT, res("hT"), sqT, res("actT"))
            hres = res("hT")
            rhs_h = lambda kc: hT[:, kc * 512:(kc + 1) * 512]
            stage(2.1)
            kc4 = kcT2.rearrange("p (t g c) -> p t g c", t=2, g=4, c=528)
            for typ in range(2):
                v_copy("dve", kc4[0:64, typ, :, 0:16], kc4[0:64, typ, :, 512:528], [res("kcT2")], [res("kcT2")])

                def evac_kc(idx, pb, pr, typ=typ):
                    v_copy("dve", kc4[0:64, typ, idx, 16:528], pb[0:64, 0:512], [pr], [res("kcT2")])
                    act(kc4[64:128, typ, idx, 0:512], pb[64:128, 0:512], AF.Copy, [pr], [res("kcT2")])
                ws_proj(wkc_d[typ:typ + 1], 1, 4, 8, rhs_h, [hres], 512, evac_kc)
            stage(2.2)
            slot_w = T % 2

            def evac_kn(idx, pb, pr):
                gp = idx % 2
                if idx < 2:
                    headnorm(pb, pr, 512, gains[:, 33:34], ktile[:, gp * 512:(gp + 1) * 512], res("ktile"))
                else:
                    headnorm(pb, pr, 512, gains[:, 33:34], kwinT[:, gp * 1024 + slot_w * 512:gp * 1024 + slot_w * 512 + 512], res("kwinT"))
            ws_proj(wkn_d, 1, 4, 8, rhs_h, [hres], 512, evac_kn)
            for gp in range(2):
                dma("sp", kslc_d[gp][:, T * TL:(T + 1) * TL], ktile[:, gp * 512:(gp + 1) * 512], [res("ktile")], [res("kslc_d")], "kst")
            stage(2.4)
            slot, wr = load_w(wv_d[0], 4096)
            vt4 = vtile.rearrange("p (g s x) -> p g s x", g=4, s=4, x=65)
            v_memset("pool", vt4[:, :, :, 64:65], 1.0, [res("vtile")])
            for s in range(4):
                pb, pr = next_bank()
                for kc in range(8):
                    mm(pb[:, 0:512], hT[:, kc * 512 + s * 128:kc * 512 + s * 128 + 128], slot[:, kc * 512:(kc + 1) * 512],
                       kc == 0, kc == 7, [wr, hres], [pr])
                v_copy("dve", vt4[:, :, s, 0:64], v3(pb[:, 0:256], 4, 64), [pr], [res("vtile")])
                act(vwin4[:, slot_w * 4 + s, :, 0:64], v3(pb[:, 256:512], 4, 64), AF.Copy, [pr], [res("vwin")])
            dma("sp", vslc_d.rearrange("g p (k x) -> p g k x", k=64, x=65)[:, :, T * 4:(T + 1) * 4, :], vt4, [res("vtile")], [res("vslc_d")], "vst")
            stage(2.6)
            ub3 = ubuf.rearrange("p (c n) -> p c n", c=8, n=514)
            v_copy("dve", ub3[:, :, 0:2], ub3[:, :, 512:514], [res("ubuf")], [res("ubuf")])
            cst = {}

            def evac_conv(idx, pb, pr):
                c, k = idx // 3, idx % 3
                if k == 0:
                    act(tmpF[0][:, :], pb[:, 0:512], AF.Copy, [pr], [res("tmpF0")])
                elif k == 1:
                    act(tmpF[1][:, :], pb[:, 0:512], AF.Copy, [pr], [res("tmpF1")])
                else:
                    v_tt("dve", ub3[:, c, 2:514], tmpF[1][:, :], pb[:, 0:512], ALU.mult, [res("tmpF1"), pr], [res("ubuf")])
                    t = tmpF[3][:, 0:512]; t3r = res("tmpF3")
                    v_ts("dve", t, ub3[:, c, 2:514], gains2[:, 16 + c:17 + c], gains2[:, 24 + c:25 + c], ALU.mult, ALU.add,
                         [res("ubuf"), cres], [t3r])
                    v_stt("dve", t, ub3[:, c, 1:513], gains2[:, 8 + c:9 + c], t, ALU.mult, ALU.add, [res("ubuf"), t3r, cres], [t3r])
                    v_stt("dve", t, ub3[:, c, 0:512], gains2[:, c:c + 1], t, ALU.mult, ALU.add, [res("ubuf"), t3r, cres], [t3r])
                    v_tt("dve", convall[:, c * 512:(c + 1) * 512], t, tmpF[0][:, :], ALU.mult, [t3r, res("tmpF0")], [res("convall")])
            ws_proj(wconv_d, 8, 3, 8, rhs_h, [hres], 512, evac_conv)
            stage(2.8)
            hid4 = hidT.rearrange("p (h g i) -> p h g i", h=2, g=4, i=32)
            hvp = hidVpad.rearrange("p (h g c) -> p h g c", h=2, g=4, c=128)
            cc0 = 32 * T + 32
            off = cc0 % 128
            ct_new = cc0 // 128
            for typ in range(2):
                slot, wr = load_w(cw1_d[typ], 4096)
                pb, pr = next_bank()
                pv = pb[:, 0:256].rearrange("p (h g i) -> p h g i", h=2, g=4, i=32)
                for g in range(4):
                    for hc in range(2):
                        for jlo in range(16):
                            mm(pv[:, hc, g, :], slot[:, jlo * 256 + hc * 128:jlo * 256 + hc * 128 + 128],
                               kc4[:, typ, g, jlo:jlo + 512:16], jlo == 0, jlo == 15, [wr, res("kcT2")], [pr])
                if typ == 0:
                    for hc in range(2):
                        act(hid4[:, hc, :, :], pv[:, hc, :, :], AF.Silu, [pr], [res("hidT")], bias=cbias[:, hc:hc + 1])
                    for gp in range(2):
                        p2, p2r = next_bank()
                        k = 0
                        for b in range(2):
                            for hc in range(2):
                                mm(p2[:, 0:32], cw2[:, b * 256 + hc * 128:b * 256 + hc * 128 + 128], hid4[:, hc, gp * 2 + b, :],
                                   k == 0, k == 3, [cres, res("hidT")], [p2r])
                                k += 1
                        headnorm(p2, p2r, 32, gains[:, 33:34], kcmpT[:, gp * 576 + cc0:gp * 576 + cc0 + 32], res("kcmpT"))
                else:
                    v_memset("pool", hidVpad[:, :], 0.0, [res("hidVpad")])
                    for hc in range(2):
                        act(hvp[:, hc, :, off:off + 32], pv[:, hc, :, :], AF.Silu, [pr], [res("hidVpad")], bias=cbias[:, 2 + hc:3 + hc])
                    p2, p2r = next_bank()
                    p2v = p2[:, 0:256].rearrange("p (g d) -> p g d", g=4, d=64)
                    for g in range(4):
                        for hc in range(2):
                            mm(p2v[:, g, :], hvp[:, hc, g, :], cw2[:, 512 + hc * 64:512 + hc * 64 + 64], hc == 0, hc == 1,
                               [cres, res("hidVpad")], [p2r])
                    v_tt("dve", cmpV4[:, ct_new, :, 0:64], cmpV4[:, ct_new, :, 0:64], p2v, ALU.add, [p2r, res("cmpV")], [res("cmpV")])
                    if T == 0:
                        v_memset("dve", cmpV4[32:33, 0, :, 0:64], 0.0, [res("cmpV")])
            stage(2.9)
            a0 = avec[:, 0:1]; a1 = avec[:, 1:2]

            def blend(dst, src, N2, srcres, dstres, eng="dve", ti=0):
                s3 = src.rearrange("p (c m two) -> p c m two", c=8, m=N2, two=2)
                d3 = dst.rearrange("p (c m) -> p c m", c=8, m=N2)
                for c in range(8):
                    t = tmpF[ti + c % 2][:, 0:N2]; tr = res("tmpF%d" % (ti + c % 2))
                    v_ts(eng, t, s3[:, c, :, 0], a0, None, ALU.mult, None, [srcres, cres], [tr])
                    if eng == "dve":
                        v_stt(eng, d3[:, c, :], s3[:, c, :, 1], a1, t, ALU.mult, ALU.add, [srcres, cres, tr], [dstres])
                    else:
                        tb = tmpF[ti + c % 2][:, N2:2 * N2]
                        v_ts(eng, tb, s3[:, c, :, 1], a1, None, ALU.mult, None, [srcres, cres], [tr])
                        v_tt(eng, d3[:, c, :], t, tb, ALU.add, [tr], [dstres])
            blend(x1own[:, :], xT[:, :], 256, xr, res("x1own"))
            blend(hown[:, :], hT, 256, hres, res("hown"), eng="dve", ti=2)
            blend(convown[:, :], convall, 256, res("convall"), res("convown"), eng="dve", ti=2)

            stage(3)
            alias(ph1 + ph2, ph2)
            ho = res("hown")
            rhs_o = lambda kc: hown[:, kc * OW:(kc + 1) * OW]
            v_memset("pool", qz[0][64:128, :], 0.0, [res("qz0")])
            v_memset("pool", qz[1][0:64, :], 0.0, [res("qz1")])

            def evac_q(idx, pb, pr):
                headnorm(pb, pr, OW, gains[:, 32:33], qT[:, idx * OW:(idx + 1) * OW], res("qT"))
                v_copy("dve", qz[0][0:64, idx * OW:(idx + 1) * OW], qT[0:64, idx * OW:(idx + 1) * OW], [res("qT")], [res("qz0")])
                act(qz[1][64:128, idx * OW:(idx + 1) * OW], qT[64:128, idx * OW:(idx + 1) * OW], AF.Copy, [res("qT")], [res("qz1")])
            ws_proj(wq_d, 2, 4, 8, rhs_o, [ho], OW, evac_q)
            for jl in range(2):
                pb, pr = PB[6], PBr[6]
                for kc in range(8):
                    mm(pb[:, 0:48], hown[:, kc * OW + jl * 128:kc * OW + jl * 128 + 128], wgate[:, kc * 48:(kc + 1) * 48],
                       kc == 0, kc == 7, [ho, cres], [pr])
                act(small[:, jl * 48:(jl + 1) * 48], pb[:, 0:48], AF.Sigmoid, [pr], [res("gates")])
            qraw = [tmpF[0], tmpF[1]]

            def evac_qm(idx, pb, pr):
                i = idx % 2
                v_copy("dve", qraw[i][:, 0:OW], pb[:, 0:OW], [pr], [res("tmpF%d" % i)])
                act(sqq[:, i * OW:(i + 1) * OW], pb[:, 0:OW], AF.Square, [pr], [res("sqq")])
                if i == 1:
                    hm = idx // 2
                    p2, p2r = PB[6], PBr[6]
                    for ii in range(2):
                        mm(p2[:, 0:OW], ones, sqq[:, ii * OW:(ii + 1) * OW], ii == 0, ii == 1, [res("sqq"), cres], [p2r])
                    rs = tmpF[2]
                    act(rs[:, 0:OW], p2[:, 0:OW], AF.Sqrt, [p2r], [res("tmpF2")], bias=EPS, scale=1.0 / 256.0)
                    op("dve", lambda e: e.reciprocal(out=rs[:, 0:OW], in_=rs[:, 0:OW]), reads=[res("tmpF2")], writes=[res("tmpF2")])
                    for ii in range(2):
                        c = hm * 2 + ii
                        v_stt("dve", qmT[:, c * OW:(c + 1) * OW], qraw[ii][:, 0:OW], gains[:, 34 + ii:35 + ii], rs[:, 0:OW],
                              ALU.mult, ALU.mult, [res("tmpF%d" % ii), res("tmpF2"), cres], [res("qmT")])
            ws_proj(wqm_d, 2, 4, 8, rhs_o, [ho], OW, evac_qm)
            def evac_mg(idx, pb, pr):
                act(sg[:, idx * OW:(idx + 1) * OW], pb[:, 0:OW], AF.Sigmoid, [pr], [res("sg")])
            ws_proj(wmg_d, 6, 4, 8, rhs_o, [ho], OW, evac_mg)
            stage(5)
            for hm in range(4):
                for mc in range(2):
                    pb, pr = next_bank((0, 1))
                    for dc in range(2):
                        c = hm * 2 + dc
                        mm(pb[:, 0:OW], kmT[:, c * 256 + mc * 128:c * 256 + mc * 128 + 128], qmT[:, c * OW:(c + 1) * OW],
                           dc == 0, dc == 1, [res("kmT"), res("qmT")], [pr])
                    act(pmT[:, mc * OW:(mc + 1) * OW], pb[:, 0:OW], AF.Exp, [pr], [res("pmT")])
                pd, pdr = PB[6], PBr[6]
                for mc in range(2):
                    mm(pd[:, 0:OW], ones, pmT[:, mc * OW:(mc + 1) * OW], mc == 0, mc == 1, [res("pmT"), cres], [pdr])
                rs = tmpF[2]
                op("dve", lambda e, pd=pd: e.reciprocal(out=rs[:, 0:OW], in_=pd[:, 0:OW]), reads=[pdr], writes=[res("tmpF2")])
                for dc in range(2):
                    pb, pr = next_bank((2, 3))
                    for mc in range(2):
                        mm(pb[:, 0:OW], vm[:, mc * 1024 + hm * 256 + dc * 128:mc * 1024 + hm * 256 + dc * 128 + 128],
                           pmT[:, mc * OW:(mc + 1) * OW], mc == 0, mc == 1, [res("vm"), res("pmT")], [pr])
                    c = hm * 2 + dc
                    v_tt("dve", omemT[:, c * OW:(c + 1) * OW], pb[:, 0:OW], rs[:, 0:OW], ALU.mult, [pr, res("tmpF2")], [res("omemT")])

            stage(6)
            for jl in range(2):
                j = 2 * T + jl
                onr = res("onsa%d" % jl)
                ona = onsa[:, jl * 1024:(jl + 1) * 1024]
                gsm = small[:, jl * 48:(jl + 1) * 48]
                w0 = 16 * j + (16 if jl == 1 else 0)
                Wn = 32 if jl == 1 else 64
                Rr = 32 * T + 64
                for g in range(4):
                    gp, beta = g // 2, g % 2
                    qg = qz[beta][:, gp * 4 * OW:(gp * 4 + 4) * OW].rearrange(
                        "p (r m) -> p r m", r=4, m=OW)[:, :, jl * 128:jl * 128 + 128]
                    qres = res("qz%d" % beta)
                    OC = [PB[2], PB[3]]; OCr = [PBr[2], PBr[3]]
                    nct = (Rr + 127) // 128
                    first = [True, True]
                    pti = 0
                    for ct in range(nct):
                        n = min(128, Rr - 128 * ct)
                        sb_, sr = next_bank((0, 1))
                        offw = w0 - 128 * ct
                        has_near = (offw < n) and (offw + Wn > 0)
                        mm(sb_[0:n, :], kcmpT[:, gp * 576 + ct * 128:gp * 576 + ct * 128 + n], qg,
                           True, not has_near, [res("kcmpT"), qres], [sr])
                        if has_near:
                            mm(sb_[0:n, :], Iext[0:Wn, 128 - offw:128 - offw + n],
                               biasC[0:Wn, jl * 2048 + g * 512:jl * 2048 + (g + 1) * 512], False, True, [cres], [sr])
                        pT = pTs[pti % 3]; pTr = res("pT%d" % (pti % 3)); pti += 1
                        act(pT[0:n, :], sb_[0:n, :], AF.Exp, [sr], [pTr])
                        for r in range(4):
                            ob = OC[r // 2]; obr = OCr[r // 2]
                            mm(ob[:, (r % 2) * 256:(r % 2) * 256 + 193], pT[0:n, r * 128:(r + 1) * 128], cmpV4[0:n, ct, g, :],
                               first[r // 2], False, [pTr, res("cmpV")], [obr], skip=True)
                            first[r // 2] = False
                    stage(6.1)
                    sc = small[:, 128:256]
                    for r in range(4):
                        ob = OC[r // 2]; obr = OCr[r // 2]
                        base = (r % 2) * 256
                        v_ts("dve", sc[:, r:r + 1], ob[:, base + 64:base + 65], 1e-30, None, ALU.max, None, [obr], [res("sc")])
                    op("dve", lambda e, sc=sc: e.reciprocal(out=sc[:, 0:4], in_=sc[:, 0:4]), reads=[res("sc")], writes=[res("sc")])
                    imp = tmpF[0][:, 0:128]
                    for r in range(4):
                        ob = OC[r // 2]; obr = OCr[r // 2]
                        base = (r % 2) * 256
                        if r == 0:
                            v_ts("dve", imp, ob[:, base + 65:base + 193], sc[:, 0:1], None, ALU.mult, None, [obr, res("sc")], [res("tmpF0")])
                        else:
                            v_stt("dve", imp, ob[:, base + 65:base + 193], sc[:, r:r + 1], imp, ALU.mult, ALU.add,
                                  [obr, res("sc"), res("tmpF0")], [res("tmpF0")])
                        h = g * 4 + r
                        v_tt("dve", sc[:, 4 + r:5 + r], sc[:, r:r + 1], gsm[:, h * 3:h * 3 + 1], ALU.mult, [res("sc"), res("gates")], [res("sc")])
                        v_ts("dve", ona[:, h * 64:(h + 1) * 64], ob[:, base:base + 64], sc[:, 4 + r:5 + r], None, ALU.mult, None,
                             [obr, res("sc")], [onr])
                    if DEBUG and T == NTILES - 1:
                        dma("sp", dbg["imp"][jl * 4 + g], imp, [res("tmpF0")], [res("dbgd")], "dbg")
                    stage(6.2)
                    score = tmpF[1][:, 0:128]
                    v_tt("dve", score, imp, addtab[:, 128 - 4 * j:256 - 4 * j], ALU.add, [res("tmpF0"), cres], [res("tmpF1")])
                    v_ts("dve", score[:, 0:1], score[:, 0:1], FORCE, None, ALU.add, None, [res("tmpF1")], [res("tmpF1")])
                    m8 = small[:, 256:272]
                    op("dve", lambda e, m8=m8, score=score: e.max(out=m8[:, 0:8], in_=score), reads=[res("tmpF1")], writes=[res("m8")])
                    t2 = tmpF[1][:, 128:256]
                    op("dve", lambda e, m8=m8, score=score, t2=t2: e.match_replace(out=t2, in_to_replace=m8[:, 0:8], in_values=score,
                                                                                  imm_value=-3.0e4),
                       reads=[res("tmpF1"), res("m8")], writes=[res("tmpF1b")])
                    op("dve", lambda e, m8=m8, t2=t2: e.max(out=m8[:, 8:16], in_=t2), reads=[res("tmpF1b")], writes=[res("m8")])
                    v_ts("dve", negm[:, 0:128], score, m8[:, 15:16], NEGM, ALU.is_lt, ALU.mult, [res("tmpF1"), res("m8")], [res("negm")])
                    stage(6.3)
                    for a in range(4):
                        op("pe", lambda e, a=a: e.transpose(PT[0:32, a * 128:(a + 1) * 128], negm[:, a * 32:(a + 1) * 32], ident),
                           reads=[res("negm"), cres], writes=[PTr])
                    n4 = negT4.rearrange("p (a r m) -> p a r m", a=4, r=4, m=128)
                    for r in range(4):
                        eng = "dve" if r % 2 == 0 else "act"
                        if eng == "dve":
                            v_copy("dve", n4[0:32, :, r, :], v3(PT[0:32, 0:512], 4, 128), [PTr], [res("negT4")])
                        else:
                            act(n4[0:32, :, r, :], v3(PT[0:32, 0:512], 4, 128), AF.Copy, [PTr], [res("negT4")])
                    stage(6.4)
                    OW_b, OWr = PB[5], PBr[5]
                    first_w = True
                    kts = [o for o in (-4, -3, -2, -1, 0, 1) if 2 * j + o >= 0]
                    for o in kts:
                        kt = 2 * j + o
                        tile_of = kt // 4
                        slotk = tile_of % 2
                        col = slotk * 512 + (kt % 4) * 128
                        sb_, sr = next_bank((0, 1))
                        has_b = o != -2
                        mm(sb_[:, :], kwinT[:, gp * 1024 + col:gp * 1024 + col + 128], qg, True, not has_b,
                           [res("kwinT"), qres], [sr])
                        if o in (-4, -3):
                            mm(sb_[:, :], ident, maskW[:, (o + 4) * 512:(o + 5) * 512], False, True, [cres], [sr])
                        elif o in (-1, 0, 1):
                            mm(sb_[:, :], ident, biasS[:, (o + 1) * 2048 + g * 512:(o + 1) * 2048 + (g + 1) * 512], False, True, [cres], [sr])
                        pT = pTs[pti % 3]; pTr = res("pT%d" % (pti % 3)); pti += 1
                        act(pT[:, :], sb_[:, :], AF.Exp, [sr], [pTr])
                        for r in range(4):
                            mm(OW_b[:, r * 65:(r + 1) * 65], pT[:, r * 128:(r + 1) * 128], vwin4[:, slotk * 4 + kt % 4, g, :],
                               first_w, False, [pTr, res("vwin")], [OWr], skip=True)
                            first_w = False
                    stage(6.5)
                    OS_b, OSr = PB[4], PBr[4]
                    first_s = True
                    nkt = 2 * j + 2
                    for c0 in range(0, nkt, 8):
                        nk = min(8, nkt - c0)
                        ri = (c0 // 8) % 2
                        kr, krr = kring[ri], res("kring%d" % ri)
                        vr, vrr = vring[ri], res("vring%d" % ri)
                        dma("sp", kr[:, 0:nk * 128], kslc_d[gp][:, c0 * 128:(c0 + nk) * 128],
                            [res("kslc_d")], [krr], "kring%d" % ri)
                        vr4 = vr.rearrange("p (s x) -> p s x", s=8, x=65)
                        dma("sp", vr[:, 0:nk * 65], vslc_d[g][:, c0 * 65:(c0 + nk) * 65],
                            [res("vslc_d")], [vrr], "vring%d" % ri)
                        for ki in range(nk):
                            kt = c0 + ki
                            o = kt - 2 * j
                            sb_, sr = next_bank((0, 1))
                            mm(sb_[:, :], kr[:, ki * 128:(ki + 1) * 128], qg, True, False, [krr, qres], [sr])
                            a = kt // 16
                            has_b = o in (-1, 0, 1)
                            mm(sb_[:, :], E32[:, (kt % 16) * 128:(kt % 16 + 1) * 128],
                               n4[0:32, a, :, :], False, not has_b, [cres, res("negT4")], [sr])
                            if has_b:
                                mm(sb_[:, :], ident, biasS[:, (o + 1) * 2048 + g * 512:(o + 1) * 2048 + (g + 1) * 512], False, True, [cres], [sr])
                            pT = pTs[pti % 3]; pTr = res("pT%d" % (pti % 3)); pti += 1
                            act(pT[:, :], sb_[:, :], AF.Exp, [sr], [pTr])
                            for r in range(4):
                                mm(OS_b[:, r * 65:(r + 1) * 65], pT[:, r * 128:(r + 1) * 128], vr4[:, ki, :],
                                   first_s, False, [pTr, vrr], [OSr], skip=True)
                                first_s = False
                    stage(6.6)
                    for (ob, obr, br) in ((OW_b, OWr, 2), (OS_b, OSr, 1)):
                        scb = small[:, 272 + br * 8:272 + br * 8 + 8]
                        o3 = ob[:, 0:260].rearrange("p (r x) -> p r x", r=4, x=65)
                        op("dve", lambda e, scb=scb, o3=o3: e.reciprocal(out=scb[:, 0:4], in_=o3[:, :, 64]), reads=[obr], writes=[res("sc")])
                        for r in range(4):
                            h = g * 4 + r
                            v_tt("dve", scb[:, 4 + r:5 + r], scb[:, r:r + 1], gsm[:, h * 3 + br:h * 3 + br + 1], ALU.mult,
                                 [res("sc"), res("gates")], [res("sc")])
                            v_stt("dve", ona[:, h * 64:(h + 1) * 64], o3[:, r, 0:64], scb[:, 4 + r:5 + r], ona[:, h * 64:(h + 1) * 64],
                                  ALU.mult, ALU.add, [obr, res("sc"), onr], [onr])
                if DEBUG and T == NTILES - 1:
                    dma("sp", dbg["onsa"][jl], ona, [onr], [res("dbgd")], "dbg")
                stage(6.7)
                v_copy("dve", onsab[:, :], ona, [onr], [res("onsab")])
                for c in range(8):
                    op("pe", lambda e, c=c: e.transpose(PT[:, c * 128:(c + 1) * 128], onsab[:, c * 128:(c + 1) * 128], ident),
                       reads=[res("onsab"), cres], writes=[PTr])
                oT3 = onsaT.rearrange("p (c m) -> p c m", c=8, m=OW)
                v_copy("dve", oT3[:, :, jl * 128:(jl + 1) * 128], v3(PT[:, 0:1024], 8, 128), [PTr], [res("onsaT")])

            stage(7)
            for c in range(8):
                sl = slice(c * OW, (c + 1) * OW)
                t = tmpF[c % 2][:, 0:OW]; tr = res("tmpF%d" % (c % 2))
                eng = "dve"
                v_tt(eng, t, sg[:, sl], onsaT[:, sl], ALU.mult, [res("sg"), res("onsaT")], [tr])
                t2_ = tmpF[c % 2][:, 256:512]
                v_tt(eng, t2_, sg[:, (8 + c) * OW:(9 + c) * OW], convown[:, sl], ALU.mult, [res("sg"), res("convown")], [tr])
                v_tt(eng, t, t, t2_, ALU.add, [tr], [tr])
                v_tt(eng, t2_, sg[:, (16 + c) * OW:(17 + c) * OW], omemT[:, sl], ALU.mult, [res("sg"), res("omemT")], [tr])
                v_tt(eng, zT[:, sl], t, t2_, ALU.add, [tr], [res("zT")])
            def evac_o(idx, pb, pr):
                v_tt("dve", x1own[:, idx * OW:(idx + 1) * OW], pb[:, 0:OW], x1own[:, idx * OW:(idx + 1) * OW], ALU.add,
                     [pr, res("x1own")], [res("x1own")])
            ws_proj(wo_d, 2, 4, 8, lambda kc: zT[:, kc * OW:(kc + 1) * OW], [res("zT")], OW, evac_o)
            alias([res("sg")], [res("act2")]); alias([res("qT")], [res("h2")]); alias([res("qmT")], [res("sq2")])
            rmsnorm(x1own[:, :], res("x1own"), 16, OW, h2, res("h2"), sq2, res("sq2"))
            ffn(w2in_d, w2out_d, h2, res("h2"), OW, act2, res("act2"), x1own, res("x1own"))
            tok = dma("sp", outT_d.rearrange("(c p) n -> p c n", p=128)[:, :, T * OW:(T + 1) * OW], v3(x1own[:, :], 8, OW),
                      [res("x1own")], [res("outd")], "outst")
            out_tokens.append(tok)

    except StopBuild:
        pass
    finals = {}
    for sm in dsems.values():
        finals[sm] = sm.n
    for k, v in []:
        finals[k] = max(finals.get(k, 0), v)
    if "dbg" in dsems:
        finals[dsems["dbg"]] = dsems["dbg"].n
    S.emit(list(finals.items()))
    return nc


def prep_inputs(inp, core):
    b, p = core // 2, core % 2
    f32 = np.float32
    g = lambda k: np.asarray(inp[k], dtype=f32)[0] if np.asarray(inp[k]).ndim >= 2 and k != "rel_bias" else np.asarray(inp[k], dtype=f32)
    x = np.asarray(inp["x"], dtype=f32); mem = np.asarray(inp["mem"], dtype=f32)
    m = {}
    m["xT"] = np.ascontiguousarray(x[b].T)
    m["memT"] = np.ascontiguousarray(mem[b].T)
    av = np.zeros((128, 2), f32); av[:, 0] = 1.0 if p == 0 else 0.0; av[:, 1] = 1.0 - av[:, 0]
    m["avec"] = av

    def ffn_in(w):
        a = w[:, :DFF].reshape(D, 22, 128); bb = w[:, DFF:].reshape(D, 22, 128)
        return ws_layout(np.stack([a, bb], axis=2).reshape(D, 44 * 128), 4)
    m["w1in"] = ffn_in(g("ffn1_w_in")); m["w1out"] = ws_layout(g("ffn1_w_out"), 1)
    m["w2in"] = ffn_in(g("ffn2_w_in")); m["w2out"] = ws_layout(g("ffn2_w_out"), 1)
    w_in = g("w_in")
    o = 0
    Wq = w_in[:, o:o + 1024]; o += 1024
    Wkv = w_in[:, o:o + 1536]; o += 1536
    Wg = w_in[:, o:o + 48]; o += 48
    Wconv = w_in[:, o:o + 3072]; o += 3072
    Wqm = w_in[:, o:o + 1024]; o += 1024
    Wmg = w_in[:, o:o + 3072]; o += 3072
    kvs = [Wkv[:, i * 256:(i + 1) * 256] for i in range(6)]
    dup = lambda W: np.concatenate([np.concatenate([W[:, gg * 64:(gg + 1) * 64]] * 2, axis=1) for gg in range(4)], axis=1)
    m["wkc"] = np.concatenate([ws_layout(dup(kvs[0]), 4), ws_layout(dup(kvs[1]), 4)], axis=0)
    m["wkn"] = ws_layout(np.concatenate([kvs[2], kvs[4]], axis=1), 4)
    m["wv"] = tm_layout(np.concatenate([kvs[3], kvs[5]], axis=1))[None]
    Wc3 = np.stack([Wconv[:, 0:1024].reshape(D, 8, 128), Wconv[:, 1024:2048].reshape(D, 8, 128),
                    Wconv[:, 2048:3072].reshape(D, 8, 128)], axis=2).reshape(D, 24 * 128)
    m["wconv"] = ws_layout(Wc3, 3)
    cols = []
    for gp in range(2):
        for r in range(4):
            for beta in range(2):
                h = (2 * gp + beta) * 4 + r
                cols.append(Wq[:, h * 64:(h + 1) * 64])
    m["wq"] = ws_layout(np.concatenate(cols, axis=1), 4)
    m["wqm"] = ws_layout(Wqm, 4)
    m["wmg"] = ws_layout(Wmg, 4)
    m["wg"] = tm_layout(Wg)
    m["wo"] = ws_layout(g("w_out"), 4)
    wmkv = g("w_mem_kv")
    m["wmk"] = ws_layout(wmkv[:, :1024], 4)
    m["wmv"] = np.stack([tm_layout(wmkv[:, 1024 + hh * 512:1024 + (hh + 1) * 512]) for hh in range(2)], axis=0)

    def w1stack(w1):
        a = w1.reshape(2, 16, 64, 256).transpose(0, 2, 1, 3).reshape(128, 16 * 256)
        return np.ascontiguousarray(a)
    m["cw1"] = np.stack([w1stack(g("cmp_w1_k")), w1stack(g("cmp_w1_v"))], axis=0)
    w2k = g("cmp_w2_k"); w2v = g("cmp_w2_v")
    cw2 = np.zeros((128, 640), f32)
    for beta in range(2):
        for hc in range(2):
            cw2[:, beta * 256 + hc * 128 + beta * 64: beta * 256 + hc * 128 + beta * 64 + 64] = w2k[hc * 128:(hc + 1) * 128]
    for hc in range(2):
        cw2[:, 512 + hc * 64:512 + (hc + 1) * 64] = w2v[hc * 128:(hc + 1) * 128]
    m["cw2"] = cw2
    pes = lambda pe: pe.reshape(2, 16, 64).transpose(0, 2, 1).reshape(128, 16)
    m["cpe"] = np.ascontiguousarray(np.concatenate([pes(g("cmp_pe_k")), pes(g("cmp_pe_v"))], axis=1))
    gains = np.zeros((128, 64), f32)
    gains[:, 0:8] = col_layout(g("ffn1_norm_g")); gains[:, 8:16] = col_layout(g("mix_norm_g"))
    gains[:, 16:24] = col_layout(g("ffn2_norm_g")); gains[:, 24:32] = col_layout(g("mem_norm_g"))
    gains[:, 32] = np.tile(g("q_norm_g"), 2); gains[:, 33] = np.tile(g("k_norm_g"), 2)
    gains[:, 34:36] = col_layout(g("mem_q_norm_g")); gains[:, 36:38] = col_layout(g("mem_k_norm_g"))
    m["gains"] = gains
    g2 = np.zeros((128, 32), f32)
    cw = g("conv_w")
    for k in range(3):
        g2[:, k * 8:(k + 1) * 8] = col_layout(cw[k])
    g2[:, 24:32] = col_layout(g("conv_b"))
    m["gains2"] = g2
    rb = np.asarray(inp["rel_bias"], dtype=f32)
    mq = np.arange(128); trel = 2 * mq + p
    kk = np.arange(128)
    GS = np.zeros((3, 128, 16, 128), f32); MS = np.zeros((3, 128, 16, 128), f32)
    for oi, o in enumerate((-1, 0, 1)):
        dist = trel[None, :] - 128 * o - kk[:, None]
        bk = rel_bucket_np(dist)
        GS[oi] = rb[bk].transpose(0, 2, 1)
        MS[oi] = np.where(dist >= 0, 0.0, NEGM)[:, None, :]
    m["GS"] = GS.reshape(3, 128, 2048); m["MS"] = MS.reshape(3, 128, 2048)
    m["R31"] = np.ascontiguousarray(np.broadcast_to(rb[31][None, :, None], (128, 16, 128))).reshape(128, 2048).astype(f32)
    MW = np.zeros((2, 128, 4, 128), f32)
    for oi, o in enumerate((-4, -3)):
        dist = trel[None, :] - 128 * o - kk[:, None]
        MW[oi] = np.where(dist < 512, 0.0, NEGM)[:, None, :]
    m["MW"] = MW.reshape(2, 128, 512)
    k64 = np.arange(64)
    GCt = np.zeros((2, 64, 16, 128), f32); MCt = np.zeros((2, 64, 16, 128), f32)
    for jp, cst in enumerate((497, 241)):
        dist = trel[None, :] - 16 * k64[:, None] + cst
        bk = rel_bucket_np(dist)
        GCt[jp] = rb[bk].transpose(0, 2, 1)
        MCt[jp] = np.where(dist >= 0, 0.0, NEGM)[:, None, :]
    m["GC"] = GCt.reshape(2, 64, 2048); m["MC"] = MCt.reshape(2, 64, 2048)
    cq = trel // 64
    xx = np.arange(256) - 128
    at = np.where(xx[None, :] > cq[:, None], -FORCE, np.where(xx[None, :] >= cq[:, None] - 1, FORCE, 0.0)).astype(f32)
    m["addtab"] = at
    cc = np.arange(640); c = cc - 33
    blk = np.arange(128)
    cs = c[:, None] * 16; ss = blk[None, :] * 64
    ov = np.maximum(np.minimum(cs + 32, ss + 64) - np.maximum(cs, ss), 0) / 32.0
    ov = np.where((c[:, None] >= 0) & (c[:, None] <= 510), ov, 0.0).astype(f32)
    m["selmap"] = np.ascontiguousarray(ov.reshape(5, 128, 128).transpose(1, 0, 2)).reshape(128, 640)
    o65 = ((c >= 0) & (c <= 510)).astype(f32).reshape(5, 128).T
    m["ones65"] = np.ascontiguousarray(np.repeat(o65[:, :, None], 4, axis=2)).reshape(128, 20)
    E32 = (np.arange(32)[:, None] == (np.arange(2048)[None, :] // 64)).astype(f32)
    m["E32"] = E32
    Iext = (np.arange(320)[None, :] == (np.arange(64)[:, None] + 128)).astype(f32)
    m["Iext"] = Iext
    cm = np.zeros((128, 384), f32)
    cm[:, 0:128] = np.eye(128)
    cm[:, 128:256] = (np.arange(128)[:, None] // 64 == np.arange(128)[None, :] // 64)
    cm[:, 256:384] = 1.0
    m["cmat"] = cm
    return m


_CACHE = {}


def kernel(**inputs):
    if "nc" not in _CACHE:
        _CACHE["nc"] = build_program()
    nc = _CACHE["nc"]
    in_maps = [prep_inputs(inputs, c) for c in range(8)]
    res = run_bass_kernel_spmd(nc, in_maps, core_ids=list(range(8)))
    x = np.asarray(inputs["x"])
    out = np.zeros(x.shape, dtype=np.float32)
    for c in range(8):
        b, p = c // 2, c % 2
        oT = np.asarray(res.results[c]["outT"])
        out[b, p::2, :] = oT.T
    _CACHE["last"] = res
    return out
```

```python
import math
from contextlib import ExitStack
from functools import partial
import numpy as np
import concourse.bass as bass
import concourse.mybir as mybir
from concourse.bass_utils import run_bass_kernel_spmd

F32 = mybir.dt.float32
BF16 = mybir.dt.bfloat16
ALU = mybir.AluOpType
AF = mybir.ActivationFunctionType
AX = mybir.AxisListType

D = 1024; SEQ = 8192; DFF = 2816; NT = 16; TL = 512; OW = 256
NEGM = -30000.0
FORCE = 1.0e4
EPS = 1e-6
import os
DEBUG = bool(int(os.environ.get('KDEBUG', '0')))
NTILES = int(os.environ.get('KNTILES', str(NT)))
STAGE = float(os.environ.get('KSTAGE', '99'))
PRECAST = bool(int(os.environ.get('KPRECAST', '1')))


class StopBuild(Exception):
    pass


def stage(n):
    if STAGE < n:
        raise StopBuild()


class Res:
    __slots__ = ("w", "r", "excl")

    def __init__(self, excl=False):
        self.w = None
        self.r = {}
        self.excl = excl


class Sem:
    def __init__(self, h):
        self.h = h
        self.n = 0


class Sched:
    def __init__(self, nc, st):
        self.nc = nc
        self.st = st
        self.E = {}
        for name in ("pe", "act", "dve", "pool", "sp"):
            self.E[name] = dict(sem=Sem(st.enter_context(nc.semaphore("sem_" + name))), ops=[], waited={})
        self.nsem = 0

    def new_sem(self):
        self.nsem += 1
        return Sem(self.st.enter_context(self.nc.semaphore("dsem%d" % self.nsem)))

    def op(self, eng, fn, reads=(), writes=(), dsem=None):
        E = self.E[eng]
        need = {}

        def add(tok):
            if tok is None:
                return
            k, v = tok
            if need.get(k, 0) < v:
                need[k] = v
        for r in reads:
            add(r.w)
            if r.excl:
                for k, v in r.r.items():
                    if k is not E["sem"]:
                        add((k, v))
        for w in writes:
            add(w.w)
            for k, v in w.r.items():
                add((k, v))
        if dsem is None:
            sem = E["sem"]; sem.n += 1; inc = 1
        else:
            sem = dsem; sem.n += 16; inc = 16
        tok = (sem, sem.n)
        waits = []
        for k, v in need.items():
            if eng == "pe" and k is E["sem"]:
                continue
            if E["waited"].get(k, 0) < v:
                E["waited"][k] = v
                waits.append((k, v))
        E["ops"].append((waits, fn, sem, inc))
        for r in reads:
            if r.r.get(sem, 0) < tok[1]:
                r.r[sem] = tok[1]
        for w in writes:
            w.w = tok
            w.r = {}
        return tok

    def emit(self, final_waits):
        nc = self.nc
        with nc.Block() as blk:
            def runner(name):
                ops = self.E[name]["ops"]

                def f(e):
                    for waits, fn, sem, inc in ops:
                        for k, v in waits:
                            e.wait_ge(k.h, v)
                        fn(e).then_inc(sem.h, inc)
                    if name == "sp":
                        for k, v in final_waits:
                            e.wait_ge(k.h, v)
                return f
            blk.tensor(runner("pe"))
            blk.scalar(runner("act"))
            blk.vector(runner("dve"))
            blk.gpsimd(runner("pool"))
            blk.sync(runner("sp"))


def alias(olds, news):
    m = {}
    for o in olds:
        if o.w is not None and m.get(o.w[0], 0) < o.w[1]:
            m[o.w[0]] = o.w[1]
        for k, v in o.r.items():
            if m.get(k, 0) < v:
                m[k] = v
    for n in news:
        n.w = None
        n.r = dict(m)


def rel_bucket_np(dist):
    n = np.maximum(dist, 0)
    nf = np.maximum(n, 1).astype(np.float32)
    large = 16 + (np.log(nf / np.float32(16)) / np.float32(math.log(8.0)) * np.float32(16)).astype(np.int32)
    large = np.minimum(large, 31)
    return np.where(n < 16, n, large)


def ws_layout(W, G):
    K, N = W.shape
    KC = K // 128
    nch = N // 128
    assert nch % G == 0
    A = W.reshape(KC, 128, nch // G, G, 128).transpose(2, 1, 3, 0, 4)
    return np.ascontiguousarray(A).reshape(nch // G, 128, G * KC * 128)


def tm_layout(W):
    K, N = W.shape
    KC = K // 128
    return np.ascontiguousarray(W.reshape(KC, 128, N).transpose(1, 0, 2)).reshape(128, KC * N)


def col_layout(v):
    return np.ascontiguousarray(v.reshape(-1, 128).T)


class Builder:
    def __init__(self):
        self.nc = bass.Bass("TRN2", target_bir_lowering=False)
        self.st = ExitStack()
        self.S = Sched(self.nc, self.st)
        self.din = {}
        self.dres = {}

    def dram_in(self, name, shape, dtype=F32):
        ap = self.nc.dram_tensor(name, list(shape), dtype, kind="ExternalInput").ap()
        self.din[name] = ap
        return ap

    def sb(self, name, shape, dtype):
        return self.st.enter_context(self.nc.sbuf_tensor("sb_" + name, list(shape), dtype))

    def ps(self, name, shape, dtype=F32):
        return self.st.enter_context(self.nc.psum_tensor(name, list(shape), dtype))


def build_program():
    B = Builder()
    nc, S, st = B.nc, B.S, B.st
    op = S.op

    xT_d = B.dram_in("xT", [D, SEQ])
    memT_d = B.dram_in("memT", [D, 256])
    avec_d = B.dram_in("avec", [128, 2])
    w1in_d = B.dram_in("w1in", [11, 128, 4 * 8 * 128])
    w1out_d = B.dram_in("w1out", [8, 128, 22 * 128])
    w2in_d = B.dram_in("w2in", [11, 128, 4 * 8 * 128])
    w2out_d = B.dram_in("w2out", [8, 128, 22 * 128])
    wkc_d = B.dram_in("wkc", [2, 128, 4 * 8 * 128])
    wkn_d = B.dram_in("wkn", [1, 128, 4 * 8 * 128])
    wv_d = B.dram_in("wv", [1, 128, 8 * 512])
    wconv_d = B.dram_in("wconv", [8, 128, 3 * 8 * 128])
    wq_d = B.dram_in("wq", [2, 128, 4 * 8 * 128])
    wqm_d = B.dram_in("wqm", [2, 128, 4 * 8 * 128])
    wmg_d = B.dram_in("wmg", [6, 128, 4 * 8 * 128])
    wg_d = B.dram_in("wg", [128, 8 * 48])
    wo_d = B.dram_in("wo", [2, 128, 4 * 8 * 128])
    wmk_d = B.dram_in("wmk", [2, 128, 4 * 8 * 128])
    wmv_d = B.dram_in("wmv", [2, 128, 8 * 512])
    cw1_d = B.dram_in("cw1", [2, 128, 16 * 256])
    cw2_d = B.dram_in("cw2", [128, 2 * 2 * 128 + 2 * 64])
    cpe_d = B.dram_in("cpe", [128, 2 * 16])
    gains_d = B.dram_in("gains", [128, 64])
    GS_d = B.dram_in("GS", [3, 128, 2048])
    R31_d = B.dram_in("R31", [128, 2048])
    MS_d = B.dram_in("MS", [3, 128, 2048])
    MW_d = B.dram_in("MW", [2, 128, 512])
    GC_d = B.dram_in("GC", [2, 64, 2048])
    MC_d = B.dram_in("MC", [2, 64, 2048])
    addtab_d = B.dram_in("addtab", [128, 256])
    selmap_d = B.dram_in("selmap", [128, 5 * 128])
    E32_d = B.dram_in("E32", [32, 2048])
    Iext_d = B.dram_in("Iext", [64, 320])
    cmat_d = B.dram_in("cmat", [128, 3 * 128])
    ones65_d = B.dram_in("ones65", [128, 5 * 4])
    outT_d = nc.dram_tensor("outT", [D, SEQ // 2], F32, kind="ExternalOutput").ap()
    kslc_d = nc.dram_tensor("kslc_s", [2, 128, SEQ], BF16, kind="Internal").ap()
    vslc_d = nc.dram_tensor("vslc_s", [4, 128, 64 * 65], BF16, kind="Internal").ap()
    dbg = {}
    if DEBUG:
        dbg["x1"] = nc.dram_tensor("dbg_x1", [128, 8 * 512], F32, kind="ExternalOutput").ap()
        dbg["onsa"] = nc.dram_tensor("dbg_onsa", [2, 128, 1024], F32, kind="ExternalOutput").ap()
        dbg["z"] = nc.dram_tensor("dbg_z", [128, 8 * 256], F32, kind="ExternalOutput").ap()
        dbg["imp"] = nc.dram_tensor("dbg_imp", [8, 128, 128], F32, kind="ExternalOutput").ap()

    avec = B.sb("avec", [128, 2], F32)
    gains = B.sb("gains", [128, 64], F32)
    cmat = B.sb("cmat", [128, 384], BF16)
    ident = cmat[:, 0:128]; BDm = cmat[:, 128:256]; ones = cmat[:, 256:384]
    E32 = B.sb("E32", [32, 2048], BF16)
    Iext = B.sb("Iext", [64, 320], BF16)
    biasS = B.sb("biasS", [128, 3 * 2048], BF16)
    maskW = B.sb("maskW", [128, 2 * 512], BF16)
    biasC = B.sb("biasC", [64, 2 * 2048], BF16)
    addtab = B.sb("addtab", [128, 256], F32)
    x1own = B.sb("x1own", [128, 8 * OW], F32)
    hown = B.sb("hown", [128, 8 * OW], BF16)
    convown = B.sb("convown", [128, 8 * OW], BF16)
    kwinT = B.sb("kwinT", [128, 2 * 1024], BF16)
    vwin = B.sb("vwin", [128, 8 * 4 * 65], BF16)
    kcmpT = B.sb("kcmpT", [128, 2 * 576], BF16)
    cmpV = B.sb("cmpV", [128, 5 * 4 * 193], BF16)
    kcT2 = B.sb("kcT2", [128, 2 * 4 * 528], BF16)
    ubuf = B.sb("ubuf", [128, 8 * 514], BF16)
    kmT = B.sb("kmT", [128, 8 * 256], BF16)
    vm = B.sb("vm", [128, 2 * 1024], BF16)
    wgate = B.sb("wgate", [128, 8 * 48], BF16)
    cw2 = B.sb("cw2", [128, 640], BF16)
    cpe = B.sb("cpe", [128, 32], BF16)
    cbias = B.sb("cbias", [128, 4], F32)
    hidVpad = B.sb("hidVpad", [128, 2 * 4 * 128], BF16)
    wslots = [B.sb("wslot%d" % i, [128, 4096], BF16) for i in range(2)]
    kring = [B.sb("kring%d" % i, [128, 1024], BF16) for i in range(2)]
    vring = [B.sb("vring%d" % i, [128, 8 * 65], BF16) for i in range(2)]
    ovF = B.sb("ovF", [128, 4096], F32)
    ovB = B.sb("ovB", [128, 26624], BF16)
    small = B.sb("small", [128, 512], F32)
    tmpF = [B.sb("tmpF%d" % i, [128, 512], F32) for i in range(4)]
    stageF = ovF[:, 0:2048]

    PB = [B.ps("pb%d" % i, [128, 512], F32) for i in range(7)]
    PT = B.ps("pt", [128, 1024], BF16)
    PBr = [Res(True) for _ in range(7)]
    PTr = Res(True)

    xT = ovF
    hT = ovB[:, 0:4096]
    actT = ovB[:, 4096:4096 + 11264]
    sqT = ovB[:, 4096:4096 + 4096]
    convall = ovB[:, 15360:15360 + 4096]
    ktile = ovB[:, 19456:19456 + 1024]
    vtile = ovB[:, 20480:20480 + 1040]
    hidT = ovB[:, 21520:21520 + 256]
    sqs = ovB[:, 21776:21776 + 512]
    onsa = ovF
    o2 = 0
    qT = ovB[:, o2:o2 + 2048]; o2 += 2048
    qmT = ovB[:, o2:o2 + 2048]; o2 += 2048
    sg = ovB[:, o2:o2 + 6144]; o2 += 6144
    zT = ovB[:, o2:o2 + 2048]; o2 += 2048
    onsaT = ovB[:, o2:o2 + 2048]; o2 += 2048
    omemT = ovB[:, o2:o2 + 2048]; o2 += 2048
    h2 = qT
    act2 = sg[:, 0:5632]
    sq2 = qmT
    sqq = ovB[:, o2:o2 + 512]; o2 += 512
    pTs = [ovB[:, o2 + i * 512:o2 + (i + 1) * 512] for i in range(3)]; o2 += 1536
    negT4 = ovB[:, o2:o2 + 2048]; o2 += 2048
    onsab = ovB[:, o2:o2 + 1024]; o2 += 1024
    negm = ovB[:, o2:o2 + 128]; o2 += 128
    pmT = ovB[:, o2:o2 + 512]; o2 += 512
    qz = [ovB[:, o2 + i * 2048:o2 + (i + 1) * 2048] for i in range(2)]; o2 += 4096
    assert o2 <= 26624, o2

    R = {}

    def res(name):
        if name not in R:
            R[name] = Res()
        return R[name]
    ph1 = [res(n) for n in ("xT", "hT", "actT", "convall", "ktile", "vtile", "hidT", "sqs")]
    ph2 = [res(n) for n in ("onsa0", "onsa1", "qT", "qmT", "sg", "zT", "onsaT", "omemT", "h2", "act2", "sq2",
                            "pT0", "pT1", "pT2", "negT4", "onsab", "negm", "pmT", "sqq", "qz0", "qz1")]

    dsems = {}

    def dsem(name):
        if name not in dsems:
            dsems[name] = S.new_sem()
        return dsems[name]

    def dma(q, out, in_, reads, writes, sem):
        return op(q, lambda e, out=out, in_=in_: e.dma_start(out=out, in_=in_), reads=reads, writes=writes, dsem=dsem(sem))

    rot = {"i": 0}

    def next_bank(pool=(0, 1, 2, 3, 4, 5)):
        i = pool[rot["i"] % len(pool)]
        rot["i"] += 1
        return PB[i], PBr[i]

    wrot = {"i": 0}
    wres = [Res(), Res()]

    def next_wslot():
        i = wrot["i"] % 2
        wrot["i"] += 1
        return wslots[i], wres[i], "wsl%d" % i

    dres = Res()

    def load_w(src, nelem):
        if isinstance(src, tuple):
            src_ap, rd = src[0], [src[1]]
        else:
            src_ap, rd = src, [dres]
        slot, r, sname = next_wslot()
        dma("pool", slot[:, 0:nelem], src_ap, rd, [r], sname)
        return slot, r

    class WS:
        def __init__(self, aps, ress):
            self.aps = aps
            self.ress = ress

        def __getitem__(self, i):
            if isinstance(i, slice):
                return WS(self.aps[i], self.ress[i])
            return (self.aps[i], self.ress[i])

    def precast(name, w_d, ngrp, nelem):
        scr = nc.dram_tensor("scr_" + name, [ngrp, 128, nelem], BF16, kind="Internal").ap()
        ress = []
        for grp in range(ngrp):
            slot, r, sname = next_wslot()
            dma("pool", slot[:, 0:nelem], w_d[grp], [dres], [r], sname)
            rr = Res()
            dma("sp", scr[grp], slot[:, 0:nelem], [r], [rr], "p" + sname)
            ress.append(rr)
        return WS([scr[g] for g in range(ngrp)], ress)

    def mm(out, lhsT, rhs, start, stop, reads, writes, skip=False):
        def f(e, out=out, lhsT=lhsT, rhs=rhs, start=start, stop=stop, skip=skip):
            if skip:
                return e.matmul(out, lhsT=lhsT, rhs=rhs, start=start, stop=stop, skip_group_check=True)
            return e.matmul(out, lhsT=lhsT, rhs=rhs, start=start, stop=stop)
        return op("pe", f, reads=reads, writes=writes)

    def act(out, in_, func, reads, writes, bias=None, scale=None):
        def f(e, out=out, in_=in_, func=func, bias=bias, scale=scale):
            kw = {}
            if bias is not None:
                kw["bias"] = bias
            if scale is not None:
                kw["scale"] = scale
            return e.activation(out=out, in_=in_, func=func, **kw)
        return op("act", f, reads=reads, writes=writes)

    def v_tt(eng, out, in0, in1, alu, reads, writes):
        return op(eng, lambda e, out=out, in0=in0, in1=in1, alu=alu: e.tensor_tensor(out=out, in0=in0, in1=in1, op=alu),
                  reads=reads, writes=writes)

    def v_ts(eng, out, in0, s1, s2, op0, op1, reads, writes):
        def f(e, out=out, in0=in0, s1=s1, s2=s2, op0=op0, op1=op1):
            if op1 is None:
                return e.tensor_scalar(out=out, in0=in0, scalar1=s1, scalar2=None, op0=op0)
            return e.tensor_scalar(out=out, in0=in0, scalar1=s1, scalar2=s2, op0=op0, op1=op1)
        return op(eng, f, reads=reads, writes=writes)

    def v_stt(eng, out, in0, scalar, in1, op0, op1, reads, writes):
        return op(eng, lambda e, out=out, in0=in0, scalar=scalar, in1=in1, op0=op0, op1=op1:
                  e.scalar_tensor_tensor(out=out, in0=in0, scalar=scalar, in1=in1, op0=op0, op1=op1),
                  reads=reads, writes=writes)

    def v_copy(eng, out, in_, reads, writes):
        return op(eng, lambda e, out=out, in_=in_: e.tensor_copy(out=out, in_=in_), reads=reads, writes=writes)

    def v_memset(eng, ap, val, writes):
        return op(eng, lambda e, ap=ap, val=val: e.memset(ap, val), reads=(), writes=writes)

    def v3(ap, a, b):
        return ap.rearrange("p (a b) -> p a b", a=a, b=b)

    GC = dict(ffn1=0, mix=8, ffn2=16, memn=24, gq8=32, gk=33, gqm=34, gkm=36, cw0=38, cw1=46, cw2=54, cb=62 - 8)
    gains2_d = B.dram_in("gains2", [128, 32])
    gains2 = B.sb("gains2", [128, 32], F32)

    cres = res("consts")
    stg = res("xT")
    for (dst, src) in ((avec[:], avec_d), (gains[:], gains_d), (gains2[:], gains2_d), (addtab[:], addtab_d)):
        dma("sp", dst, src, [dres], [cres], "cst")
    for (dst, src) in ((cmat[:], cmat_d), (E32[:], E32_d), (Iext[:], Iext_d), (wgate[:], wg_d), (cw2[:], cw2_d), (cpe[:], cpe_d)):
        dma("pool", dst, src, [dres], [cres], "cstp")
    v_ts("dve", gains[:, 32:33], gains[:, 32:33], 0.125, None, ALU.mult, None, [cres], [cres])
    v_ts("dve", gains[:, 34:36], gains[:, 34:36], 1.0 / 16.0, None, ALU.mult, None, [cres], [cres])
    for o in range(3):
        dma("sp", stageF[:, :], GS_d[o], [dres], [stg], "stg")
        dma("sp", tmpF[0][:, :], R31_d[:, 0:512], [dres], [res("tmpF0")], "stg2")
        for q in range(4):
            if q > 0:
                dma("sp", tmpF[0][:, :], R31_d[:, q * 512:(q + 1) * 512], [dres], [res("tmpF0")], "stg2")
            v_tt("dve", stageF[:, q * 512:(q + 1) * 512], stageF[:, q * 512:(q + 1) * 512], tmpF[0][:, :], ALU.subtract,
                 [stg, res("tmpF0")], [stg])
            dma("sp", tmpF[1][:, :], MS_d[o][:, q * 512:(q + 1) * 512], [dres], [res("tmpF1")], "stg3")
            v_tt("dve", biasS[:, o * 2048 + q * 512:o * 2048 + (q + 1) * 512], stageF[:, q * 512:(q + 1) * 512], tmpF[1][:, :],
                 ALU.add, [stg, res("tmpF1")], [cres])
    for o in range(2):
        dma("sp", tmpF[1][:, :], MW_d[o], [dres], [res("tmpF1")], "stg3")
        v_copy("dve", maskW[:, o * 512:(o + 1) * 512], tmpF[1][:, :], [res("tmpF1")], [cres])
    for jp in range(2):
        dma("sp", stageF[0:64, :], GC_d[jp], [dres], [stg], "stg")
        for q in range(4):
            dma("sp", tmpF[0][0:64, :], R31_d[0:64, q * 512:(q + 1) * 512], [dres], [res("tmpF0")], "stg2")
            v_tt("dve", stageF[0:64, q * 512:(q + 1) * 512], stageF[0:64, q * 512:(q + 1) * 512], tmpF[0][0:64, :],
                 ALU.subtract, [stg, res("tmpF0")], [stg])
            dma("sp", tmpF[1][0:64, :], MC_d[jp][:, q * 512:(q + 1) * 512], [dres], [res("tmpF1")], "stg3")
            v_tt("dve", biasC[:, jp * 2048 + q * 512:jp * 2048 + (q + 1) * 512], stageF[0:64, q * 512:(q + 1) * 512],
                 tmpF[1][0:64, :], ALU.add, [stg, res("tmpF1")], [cres])
    v_memset("pool", kwinT[:, :], 0.0, [res("kwinT")])
    v_memset("pool", vwin[:, :], 0.0, [res("vwin")])
    v_memset("pool", kcmpT[:, :], 0.0, [res("kcmpT")])
    v_memset("pool", kcT2[:, :], 0.0, [res("kcT2")])
    v_memset("pool", ubuf[:, :], 0.0, [res("ubuf")])
    v_memset("pool", hidVpad[:, :], 0.0, [res("hidVpad")])
    v_memset("pool", ovB[:, 20480:20480 + 1040], 1.0, [res("vtile")])
    cmpV4 = cmpV.rearrange("p (c g x) -> p c g x", c=5, g=4, x=193)
    v_memset("pool", cmpV[:, :], 0.0, [res("cmpV")])
    vwin4 = vwin.rearrange("p (s g x) -> p s g x", s=8, g=4, x=65)
    v_memset("pool", vwin4[:, :, :, 64:65], 1.0, [res("vwin")])
    dma("sp", stageF[:, 0:640], selmap_d, [dres], [stg], "stg")
    for g in range(4):
        v_copy("dve", cmpV4[:, :, g, 65:193], v3(stageF[:, 0:640], 5, 128), [stg], [res("cmpV")])
    dma("sp", tmpF[2][:, 0:20], ones65_d, [dres], [res("tmpF2")], "stg4")
    v_copy("dve", cmpV4[:, :, :, 64], v3(tmpF[2][:, 0:20], 5, 4), [res("tmpF2")], [res("cmpV")])

    if PRECAST:
        w1in_d = precast("w1in", w1in_d, 11, 4096)
        w1out_d = precast("w1out", w1out_d, 8, 2816)
        wkc_d = precast("wkc", wkc_d, 2, 4096)
        wkn_d = precast("wkn", wkn_d, 1, 4096)
        wv_d = precast("wv", wv_d, 1, 4096)
        wconv_d = precast("wconv", wconv_d, 8, 3072)
        cw1_d = precast("cw1", cw1_d, 2, 4096)
        wq_d = precast("wq", wq_d, 2, 4096)
        wqm_d = precast("wqm", wqm_d, 2, 4096)
        wmg_d = precast("wmg", wmg_d, 6, 4096)
        wo_d = precast("wo", wo_d, 2, 4096)
        w2in_d = precast("w2in", w2in_d, 11, 4096)
        w2out_d = precast("w2out", w2out_d, 8, 2816)

    def rmsnorm(src, src_res, gcol, N, dst, dst_res, sq, sq_res, nfeat_inv=1.0 / 1024.0):
        act(sq, src, AF.Square, [src_res], [sq_res])
        pb, pr = next_bank()
        for c in range(8):
            mm(pb[:, 0:N], ones, sq[:, c * N:(c + 1) * N], c == 0, c == 7, [sq_res, cres], [pr])
        rs = tmpF[2]
        act(rs[:, 0:N], pb[:, 0:N], AF.Sqrt, [pr], [res("tmpF2")], bias=EPS, scale=nfeat_inv)
        op("dve", lambda e: e.reciprocal(out=rs[:, 0:N], in_=rs[:, 0:N]), reads=[res("tmpF2")], writes=[res("tmpF2")])
        for c in range(8):
            v_stt("dve", dst[:, c * N:(c + 1) * N], src[:, c * N:(c + 1) * N], gains[:, gcol + c:gcol + c + 1], rs[:, 0:N],
                  ALU.mult, ALU.mult, [src_res, res("tmpF2"), cres], [dst_res])

    def ws_proj(w_d, ngrp, G, KC, rhs_fn, rhs_res, N, evac, pool=(0, 1, 2, 3, 4, 5)):
        for grp in range(ngrp):
            slot, wr = load_w(w_d[grp], G * KC * 128)
            for gi in range(G):
                pb, pr = next_bank(pool)
                for kc in range(KC):
                    base = (gi * KC + kc) * 128
                    mm(pb[:, 0:N], slot[:, base:base + 128], rhs_fn(kc), kc == 0, kc == KC - 1, [wr] + rhs_res, [pr])
                evac(grp * G + gi, pb, pr)

    def ffn(w_in_d, w_out_d, h, h_res, N, a_buf, a_res, xres_ap, xres_res):
        state = {}

        def evac_in(idx, pb, pr):
            f = idx // 2
            if idx % 2 == 0:
                t = tmpF[f % 2]
                act(t[:, 0:N], pb[:, 0:N], AF.Silu, [pr], [res("tmpF%d" % (f % 2))])
                state["t"] = (t, res("tmpF%d" % (f % 2)))
            else:
                t, tr = state["t"]
                v_tt("dve", a_buf[:, f * N:(f + 1) * N], t[:, 0:N], pb[:, 0:N], ALU.mult, [tr, pr], [a_res])
        ws_proj(w_in_d, 11, 4, 8, lambda kc: h[:, kc * N:(kc + 1) * N], [h_res], N, evac_in)

        def evac_out(idx, pb, pr):
            v_stt("dve", xres_ap[:, idx * N:(idx + 1) * N], pb[:, 0:N], 0.5, xres_ap[:, idx * N:(idx + 1) * N],
                  ALU.mult, ALU.add, [pr, xres_res], [xres_res])
        ws_proj(w_out_d, 8, 1, 22, lambda kc: a_buf[:, kc * N:(kc + 1) * N], [a_res], N, evac_out)

    def headnorm(pb, pr, N, gcol_ap, dst, dst_res, nrows=128, inv=1.0 / 64.0, reduce_mat=None):
        sq = sqs
        act(sq[0:nrows, 0:N], pb[0:nrows, 0:N], AF.Square, [pr], [res("sqs")])
        p2, p2r = PB[6], PBr[6]
        mm(p2[0:nrows, 0:N], (BDm if reduce_mat is None else reduce_mat)[0:nrows, 0:nrows], sq[0:nrows, 0:N], True, True,
           [res("sqs"), cres], [p2r])
        rs = tmpF[2]
        act(rs[0:nrows, 0:N], p2[0:nrows, 0:N], AF.Sqrt, [p2r], [res("tmpF2")], bias=EPS, scale=inv)
        op("dve", lambda e: e.reciprocal(out=rs[0:nrows, 0:N], in_=rs[0:nrows, 0:N]), reads=[res("tmpF2")], writes=[res("tmpF2")])
        v_stt("dve", dst, pb[0:nrows, 0:N], gcol_ap, rs[0:nrows, 0:N], ALU.mult, ALU.mult, [pr, res("tmpF2"), cres], [dst_res])

    try:
        stage(1)
        memx = ovF[:, 0:2048]
        dma("sp", v3(memx, 8, 256), memT_d.rearrange("(c p) n -> p c n", p=128), [dres], [res("xT")], "xin")
        memn = ovB[:, 0:2048]
        rmsnorm(memx, res("xT"), 24, 256, memn, res("hT"), ovB[:, 4096:4096 + 2048], res("actT"))
        kraw = [tmpF[0], tmpF[1]]

        def evac_mk(idx, pb, pr):
            i = idx % 2
            v_copy("dve", kraw[i][:, 0:256], pb[:, 0:256], [pr], [res("tmpF%d" % i)])
            act(sq2_m[:, i * 256:(i + 1) * 256], pb[:, 0:256], AF.Square, [pr], [res("sqs")])
            if i == 1:
                hm = idx // 2
                p2, p2r = PB[6], PBr[6]
                for ii in range(2):
                    mm(p2[:, 0:256], ones, sq2_m[:, ii * 256:(ii + 1) * 256], ii == 0, ii == 1, [res("sqs"), cres], [p2r])
                rs = tmpF[2]
                act(rs[:, 0:256], p2[:, 0:256], AF.Sqrt, [p2r], [res("tmpF2")], bias=EPS, scale=1.0 / 256.0)
                op("dve", lambda e: e.reciprocal(out=rs[:, 0:256], in_=rs[:, 0:256]), reads=[res("tmpF2")], writes=[res("tmpF2")])
                for ii in range(2):
                    c = hm * 2 + ii
                    v_stt("dve", kmT[:, c * 256:(c + 1) * 256], kraw[ii][:, 0:256], gains[:, 36 + ii:37 + ii], rs[:, 0:256],
                          ALU.mult, ALU.mult, [res("tmpF%d" % ii), res("tmpF2"), cres], [res("kmT")])
        sq2_m = sqs
        ws_proj(wmk_d, 2, 4, 8, lambda kc: memn[:, kc * 256:(kc + 1) * 256], [res("hT")], 256, evac_mk)
        for half in range(2):
            slot, wr = load_w(wmv_d[half], 4096)
            for mc in range(2):
                pb, pr = next_bank()
                for kc in range(8):
                    mm(pb[:, 0:512], memn[:, kc * 256 + mc * 128:kc * 256 + mc * 128 + 128], slot[:, kc * 512:(kc + 1) * 512],
                       kc == 0, kc == 7, [wr, res("hT")], [pr])
                act(vm[:, mc * 1024 + half * 512:mc * 1024 + half * 512 + 512], pb[:, 0:512], AF.Copy, [pr], [res("vm")])
        for typ in range(2):
            slot, wr = load_w(cw1_d[typ], 4096)
            pb, pr = PB[6], PBr[6]
            for hc in range(2):
                for jlo in range(16):
                    mm(pb[:, hc:hc + 1], slot[:, jlo * 256 + hc * 128:jlo * 256 + hc * 128 + 128],
                       cpe[:, typ * 16 + jlo:typ * 16 + jlo + 1], jlo == 0, jlo == 15, [wr, cres], [pr])
            v_copy("dve", cbias[:, typ * 2:typ * 2 + 2], pb[:, 0:2], [pr], [cres])

        stage(2)
        out_tokens = []
        for T in range(NTILES):
            alias(ph2 + ph1, ph1)
            xr = res("xT")
            dma("sp", v3(xT[:, :], 8, 512), xT_d.rearrange("(c p) n -> p c n", p=128)[:, :, T * TL:(T + 1) * TL], [dres], [xr], "xin")
            rmsnorm(xT[:, :], xr, 0, 512, hT, res("hT"), sqT, res("actT"))
            ffn(w1in_d, w1out_d, hT, res("hT"), 512, actT, res("actT"), xT, xr)
            if DEBUG and T == NTILES - 1:
                dma("sp", dbg["x1"], xT[:, :], [xr], [res("dbgd")], "dbg")
            rmsnorm(xT[:, :], xr, 8, 512, hT, res("hT"), sqT, res("actT"))
            hres = res("hT")
            rhs_h = lambda kc: hT[:, kc * 512:(kc + 1) * 512]
            stage(2.1)
            kc4 = kcT2.rearrange("p (t g c) -> p t g c", t=2, g=4, c=528)
            for typ in range(2):
                v_copy("dve", kc4[0:64, typ, :, 0:16], kc4[0:64, typ, :, 512:528], [res("kcT2")], [res("kcT2")])

                def evac_kc(idx, pb, pr, typ=typ):
                    v_copy("dve", kc4[0:64, typ, idx, 16:528], pb[0:64, 0:512], [pr], [res("kcT2")])
                    act(kc4[64:128, typ, idx, 0:512], pb[64:128, 0:512], AF.Copy, [pr], [res("kcT2")])
                ws_proj(wkc_d[typ:typ + 1], 1, 4, 8, rhs_h, [hres], 512, evac_kc)
            stage(2.2)
            slot_w = T % 2

            def evac_kn(idx, pb, pr):
                gp = idx % 2
                if idx < 2:
                    headnorm(pb, pr, 512, gains[:, 33:34], ktile[:, gp * 512:(gp + 1) * 512], res("ktile"))
                else:
                    headnorm(pb, pr, 512, gains[:, 33:34], kwinT[:, gp * 1024 + slot_w * 512:gp * 1024 + slot_w * 512 + 512], res("kwinT"))
            ws_proj(wkn_d, 1, 4, 8, rhs_h, [hres], 512, evac_kn)
            for gp in range(2):
                dma("sp", kslc_d[gp][:, T * TL:(T + 1) * TL], ktile[:, gp * 512:(gp + 1) * 512], [res("ktile")], [res("kslc_d")], "kst")
            stage(2.4)
            slot, wr = load_w(wv_d[0], 4096)
            vt4 = vtile.rearrange("p (g s x) -> p g s x", g=4, s=4, x=65)
            v_memset("pool", vt4[:, :, :, 64:65], 1.0, [res("vtile")])
            for s in range(4):
                pb, pr = next_bank()
                for kc in range(8):
                    mm(pb[:, 0:512], hT[:, kc * 512 + s * 128:kc * 512 + s * 128 + 128], slot[:, kc * 512:(kc + 1) * 512],
                       kc == 0, kc == 7, [wr, hres], [pr])
                v_copy("dve", vt4[:, :, s, 0:64], v3(pb[:, 0:256], 4, 64), [pr], [res("vtile")])
                act(vwin4[:, slot_w * 4 + s, :, 0:64], v3(pb[:, 256:512], 4, 64), AF.Copy, [pr], [res("vwin")])
            dma("sp", vslc_d.rearrange("g p (k x) -> p g k x", k=64, x=65)[:, :, T * 4:(T + 1) * 4, :], vt4, [res("vtile")], [res("vslc_d")], "vst")
            stage(2.6)
            ub3 = ubuf.rearrange("p (c n) -> p c n", c=8, n=514)
            v_copy("dve", ub3[:, :, 0:2], ub3[:, :, 512:514], [res("ubuf")], [res("ubuf")])
            cst = {}

            def evac_conv(idx, pb, pr):
                c, k = idx // 3, idx % 3
                if k == 0:
                    act(tmpF[0][:, :], pb[:, 0:512], AF.Copy, [pr], [res("tmpF0")])
                elif k == 1:
                    act(tmpF[1][:, :], pb[:, 0:512], AF.Copy, [pr], [res("tmpF1")])
                else:
                    v_tt("dve", ub3[:, c, 2:514], tmpF[1][:, :], pb[:, 0:512], ALU.mult, [res("tmpF1"), pr], [res("ubuf")])
                    t = tmpF[3][:, 0:512]; t3r = res("tmpF3")
                    v_ts("dve", t, ub3[:, c, 2:514], gains2[:, 16 + c:17 + c], gains2[:, 24 + c:25 + c], ALU.mult, ALU.add,
                         [res("ubuf"), cres], [t3r])
                    v_stt("dve", t, ub3[:, c, 1:513], gains2[:, 8 + c:9 + c], t, ALU.mult, ALU.add, [res("ubuf"), t3r, cres], [t3r])
                    v_stt("dve", t, ub3[:, c, 0:512], gains2[:, c:c + 1], t, ALU.mult, ALU.add, [res("ubuf"), t3r, cres], [t3r])
                    v_tt("dve", convall[:, c * 512:(c + 1) * 512], t, tmpF[0][:, :], ALU.mult, [t3r, res("tmpF0")], [res("convall")])
            ws_proj(wconv_d, 8, 3, 8, rhs_h, [hres], 512, evac_conv)
            stage(2.8)
            hid4 = hidT.rearrange("p (h g i) -> p h g i", h=2, g=4, i=32)
            hvp = hidVpad.rearrange("p (h g c) -> p h g c", h=2, g=4, c=128)
            cc0 = 32 * T + 32
            off = cc0 % 128
            ct_new = cc0 // 128
            for typ in range(2):
                slot, wr = load_w(cw1_d[typ], 4096)
                pb, pr = next_bank()
                pv = pb[:, 0:256].rearrange("p (h g i) -> p h g i", h=2, g=4, i=32)
                for g in range(4):
                    for hc in range(2):
                        for jlo in range(16):
                            mm(pv[:, hc, g, :], slot[:, jlo * 256 + hc * 128:jlo * 256 + hc * 128 + 128],
                               kc4[:, typ, g, jlo:jlo + 512:16], jlo == 0, jlo == 15, [wr, res("kcT2")], [pr])
                if typ == 0:
                    for hc in range(2):
                        act(hid4[:, hc, :, :], pv[:, hc, :, :], AF.Silu, [pr], [res("hidT")], bias=cbias[:, hc:hc + 1])
                    for gp in range(2):
                        p2, p2r = next_bank()
                        k = 0
                        for b in range(2):
                            for hc in range(2):
                                mm(p2[:, 0:32], cw2[:, b * 256 + hc * 128:b * 256 + hc * 128 + 128], hid4[:, hc, gp * 2 + b, :],
                                   k == 0, k == 3, [cres, res("hidT")], [p2r])
                                k += 1
                        headnorm(p2, p2r, 32, gains[:, 33:34], kcmpT[:, gp * 576 + cc0:gp * 576 + cc0 + 32], res("kcmpT"))
                else:
                    v_memset("pool", hidVpad[:, :], 0.0, [res("hidVpad")])
                    for hc in range(2):
                        act(hvp[:, hc, :, off:off + 32], pv[:, hc, :, :], AF.Silu, [pr], [res("hidVpad")], bias=cbias[:, 2 + hc:3 + hc])
                    p2, p2r = next_bank()
                    p2v = p2[:, 0:256].rearrange("p (g d) -> p g d", g=4, d=64)
                    for g in range(4):
                        for hc in range(2):
                            mm(p2v[:, g, :], hvp[:, hc, g, :], cw2[:, 512 + hc * 64:512 + hc * 64 + 64], hc == 0, hc == 1,
                               [cres, res("hidVpad")], [p2r])
                    v_tt("dve", cmpV4[:, ct_new, :, 0:64], cmpV4[:, ct_new, :, 0:64], p2v, ALU.add, [p2r, res("cmpV")], [res("cmpV")])
                    if T == 0:
                        v_memset("dve", cmpV4[32:33, 0, :, 0:64], 0.0, [res("cmpV")])
            stage(2.9)
            a0 = avec[:, 0:1]; a1 = avec[:, 1:2]

            def blend(dst, src, N2, srcres, dstres, eng="dve", ti=0):
                s3 = src.rearrange("p (c m two) -> p c m two", c=8, m=N2, two=2)
                d3 = dst.rearrange("p (c m) -> p c m", c=8, m=N2)
                for c in range(8):
                    t = tmpF[ti + c % 2][:, 0:N2]; tr = res("tmpF%d" % (ti + c % 2))
                    v_ts(eng, t, s3[:, c, :, 0], a0, None, ALU.mult, None, [srcres, cres], [tr])
                    if eng == "dve":
                        v_stt(eng, d3[:, c, :], s3[:, c, :, 1], a1, t, ALU.mult, ALU.add, [srcres, cres, tr], [dstres])
                    else:
                        tb = tmpF[ti + c % 2][:, N2:2 * N2]
                        v_ts(eng, tb, s3[:, c, :, 1], a1, None, ALU.mult, None, [srcres, cres], [tr])
                        v_tt(eng, d3[:, c, :], t, tb, ALU.add, [tr], [dstres])
            blend(x1own[:, :], xT[:, :], 256, xr, res("x1own"))
            blend(hown[:, :], hT, 256, hres, res("hown"), eng="dve", ti=2)
            blend(convown[:, :], convall, 256, res("convall"), res("convown"), eng="dve", ti=2)

            stage(3)
            alias(ph1 + ph2, ph2)
            ho = res("hown")
            rhs_o = lambda kc: hown[:, kc * OW:(kc + 1) * OW]
            v_memset("pool", qz[0][64:128, :], 0.0, [res("qz0")])
            v_memset("pool", qz[1][0:64, :], 0.0, [res("qz1")])

            def evac_q(idx, pb, pr):
                headnorm(pb, pr, OW, gains[:, 32:33], qT[:, idx * OW:(idx + 1) * OW], res("qT"))
                v_copy("dve", qz[0][0:64, idx * OW:(idx + 1) * OW], qT[0:64, idx * OW:(idx + 1) * OW], [res("qT")], [res("qz0")])
                act(qz[1][64:128, idx * OW:(idx + 1) * OW], qT[64:128, idx * OW:(idx + 1) * OW], AF.Copy, [res("qT")], [res("qz1")])
            ws_proj(wq_d, 2, 4, 8, rhs_o, [ho], OW, evac_q)
            for jl in range(2):
                pb, pr = PB[6], PBr[6]
                for kc in range(8):
                    mm(pb[:, 0:48], hown[:, kc * OW + jl * 128:kc * OW + jl * 128 + 128], wgate[:, kc * 48:(kc + 1) * 48],
                       kc == 0, kc == 7, [ho, cres], [pr])
                act(small[:, jl * 48:(jl + 1) * 48], pb[:, 0:48], AF.Sigmoid, [pr], [res("gates")])
            qraw = [tmpF[0], tmpF[1]]

            def evac_qm(idx, pb, pr):
                i = idx % 2
                v_copy("dve", qraw[i][:, 0:OW], pb[:, 0:OW], [pr], [res("tmpF%d" % i)])
                act(sqq[:, i * OW:(i + 1) * OW], pb[:, 0:OW], AF.Square, [pr], [res("sqq")])
                if i == 1:
                    hm = idx // 2
                    p2, p2r = PB[6], PBr[6]
                    for ii in range(2):
                        mm(p2[:, 0:OW], ones, sqq[:, ii * OW:(ii + 1) * OW], ii == 0, ii == 1, [res("sqq"), cres], [p2r])
                    rs = tmpF[2]
                    act(rs[:, 0:OW], p2[:, 0:OW], AF.Sqrt, [p2r], [res("tmpF2")], bias=EPS, scale=1.0 / 256.0)
                    op("dve", lambda e: e.reciprocal(out=rs[:, 0:OW], in_=rs[:, 0:OW]), reads=[res("tmpF2")], writes=[res("tmpF2")])
                    for ii in range(2):
                        c = hm * 2 + ii
                        v_stt("dve", qmT[:, c * OW:(c + 1) * OW], qraw[ii][:, 0:OW], gains[:, 34 + ii:35 + ii], rs[:, 0:OW],
                              ALU.mult, ALU.mult, [res("tmpF%d" % ii), res("tmpF2"), cres], [res("qmT")])
            ws_proj(wqm_d, 2, 4, 8, rhs_o, [ho], OW, evac_qm)
            def evac_mg(idx, pb, pr):
                act(sg[:, idx * OW:(idx + 1) * OW], pb[:, 0:OW], AF.Sigmoid, [pr], [res("sg")])
            ws_proj(wmg_d, 6, 4, 8, rhs_o, [ho], OW, evac_mg)
            stage(5)
            for hm in range(4):
                for mc in range(2):
                    pb, pr = next_bank((0, 1))
                    for dc in range(2):
                        c = hm * 2 + dc
                        mm(pb[:, 0:OW], kmT[:, c * 256 + mc * 128:c * 256 + mc * 128 + 128], qmT[:, c * OW:(c + 1) * OW],
                           dc == 0, dc == 1, [res("kmT"), res("qmT")], [pr])
                    act(pmT[:, mc * OW:(mc + 1) * OW], pb[:, 0:OW], AF.Exp, [pr], [res("pmT")])
                pd, pdr = PB[6], PBr[6]
                for mc in range(2):
                    mm(pd[:, 0:OW], ones, pmT[:, mc * OW:(mc + 1) * OW], mc == 0, mc == 1, [res("pmT"), cres], [pdr])
                rs = tmpF[2]
                op("dve", lambda e, pd=pd: e.reciprocal(out=rs[:, 0:OW], in_=pd[:, 0:OW]), reads=[pdr], writes=[res("tmpF2")])
                for dc in range(2):
                    pb, pr = next_bank((2, 3))
                    for mc in range(2):
                        mm(pb[:, 0:OW], vm[:, mc * 1024 + hm * 256 + dc * 128:mc * 1024 + hm * 256 + dc * 128 + 128],
                           pmT[:, mc * OW:(mc + 1) * OW], mc == 0, mc == 1, [res("vm"), res("pmT")], [pr])
                    c = hm * 2 + dc
                    v_tt("dve", omemT[:, c * OW:(c + 1) * OW], pb[:, 0:OW], rs[:, 0:OW], ALU.mult, [pr, res("tmpF2")], [res("omemT")])

            stage(6)
            for jl in range(2):
                j = 2 * T + jl
                onr = res("onsa%d" % jl)
                ona = onsa[:, jl * 1024:(jl + 1) * 1024]
                gsm = small[:, jl * 48:(jl + 1) * 48]
                w0 = 16 * j + (16 if jl == 1 else 0)
                Wn = 32 if jl == 1 else 64
                Rr = 32 * T + 64
                for g in range(4):
                    gp, beta = g // 2, g % 2
                    qg = qz[beta][:, gp * 4 * OW:(gp * 4 + 4) * OW].rearrange(
                        "p (r m) -> p r m", r=4, m=OW)[:, :, jl * 128:jl * 128 + 128]
                    qres = res("qz%d" % beta)
                    OC = [PB[2], PB[3]]; OCr = [PBr[2], PBr[3]]
                    nct = (Rr + 127) // 128
                    first = [True, True]
                    pti = 0
                    for ct in range(nct):
                        n = min(128, Rr - 128 * ct)
                        sb_, sr = next_bank((0, 1))
                        offw = w0 - 128 * ct
                        has_near = (offw < n) and (offw + Wn > 0)
                        mm(sb_[0:n, :], kcmpT[:, gp * 576 + ct * 128:gp * 576 + ct * 128 + n], qg,
                           True, not has_near, [res("kcmpT"), qres], [sr])
                        if has_near:
                            mm(sb_[0:n, :], Iext[0:Wn, 128 - offw:128 - offw + n],
                               biasC[0:Wn, jl * 2048 + g * 512:jl * 2048 + (g + 1) * 512], False, True, [cres], [sr])
                        pT = pTs[pti % 3]; pTr = res("pT%d" % (pti % 3)); pti += 1
                        act(pT[0:n, :], sb_[0:n, :], AF.Exp, [sr], [pTr])
                        for r in range(4):
                            ob = OC[r // 2]; obr = OCr[r // 2]
                            mm(ob[:, (r % 2) * 256:(r % 2) * 256 + 193], pT[0:n, r * 128:(r + 1) * 128], cmpV4[0:n, ct, g, :],
                               first[r // 2], False, [pTr, res("cmpV")], [obr], skip=True)
                            first[r // 2] = False
                    stage(6.1)
                    sc = small[:, 128:256]
                    for r in range(4):
                        ob = OC[r // 2]; obr = OCr[r // 2]
                        base = (r % 2) * 256
                        v_ts("dve", sc[:, r:r + 1], ob[:, base + 64:base + 65], 1e-30, None, ALU.max, None, [obr], [res("sc")])
                    op("dve", lambda e, sc=sc: e.reciprocal(out=sc[:, 0:4], in_=sc[:, 0:4]), reads=[res("sc")], writes=[res("sc")])
                    imp = tmpF[0][:, 0:128]
                    for r in range(4):
                        ob = OC[r // 2]; obr = OCr[r // 2]
                        base = (r % 2) * 256
                        if r == 0:
                            v_ts("dve", imp, ob[:, base + 65:base + 193], sc[:, 0:1], None, ALU.mult, None, [obr, res("sc")], [res("tmpF0")])
                        else:
                            v_stt("dve", imp, ob[:, base + 65:base + 193], sc[:, r:r + 1], imp, ALU.mult, ALU.add,
                                  [obr, res("sc"), res("tmpF0")], [res("tmpF0")])
                        h = g * 4 + r
                        v_tt("dve", sc[:, 4 + r:5 + r], sc[:, r:r + 1], gsm[:, h * 3:h * 3 + 1], ALU.mult, [res("sc"), res("gates")], [res("sc")])
                        v_ts("dve", ona[:, h * 64:(h + 1) * 64], ob[:, base:base + 64], sc[:, 4 + r:5 + r], None, ALU.mult, None,
                             [obr, res("sc")], [onr])
                    if DEBUG and T == NTILES - 1:
                        dma("sp", dbg["imp"][jl * 4 + g], imp, [res("tmpF0")], [res("dbgd")], "dbg")
                    stage(6.2)
                    score = tmpF[1][:, 0:128]
                    v_tt("dve", score, imp, addtab[:, 128 - 4 * j:256 - 4 * j], ALU.add, [res("tmpF0"), cres], [res("tmpF1")])
                    v_ts("dve", score[:, 0:1], score[:, 0:1], FORCE, None, ALU.add, None, [res("tmpF1")], [res("tmpF1")])
                    m8 = small[:, 256:272]
                    op("dve", lambda e, m8=m8, score=score: e.max(out=m8[:, 0:8], in_=score), reads=[res("tmpF1")], writes=[res("m8")])
                    t2 = tmpF[1][:, 128:256]
                    op("dve", lambda e, m8=m8, score=score, t2=t2: e.match_replace(out=t2, in_to_replace=m8[:, 0:8], in_values=score,
                                                                                  imm_value=-3.0e4),
                       reads=[res("tmpF1"), res("m8")], writes=[res("tmpF1b")])
                    op("dve", lambda e, m8=m8, t2=t2: e.max(out=m8[:, 8:16], in_=t2), reads=[res("tmpF1b")], writes=[res("m8")])
                    v_ts("dve", negm[:, 0:128], score, m8[:, 15:16], NEGM, ALU.is_lt, ALU.mult, [res("tmpF1"), res("m8")], [res("negm")])
                    stage(6.3)
                    for a in range(4):
                        op("pe", lambda e, a=a: e.transpose(PT[0:32, a * 128:(a + 1) * 128], negm[:, a * 32:(a + 1) * 32], ident),
                           reads=[res("negm"), cres], writes=[PTr])
                    n4 = negT4.rearrange("p (a r m) -> p a r m", a=4, r=4, m=128)
                    for r in range(4):
                        eng = "dve" if r % 2 == 0 else "act"
                        if eng == "dve":
                            v_copy("dve", n4[0:32, :, r, :], v3(PT[0:32, 0:512], 4, 128), [PTr], [res("negT4")])
                        else:
                            act(n4[0:32, :, r, :], v3(PT[0:32, 0:512], 4, 128), AF.Copy, [PTr], [res("negT4")])
                    stage(6.4)
                    OW_b, OWr = PB[5], PBr[5]
                    first_w = True
                    kts = [o for o in (-4, -3, -2, -1, 0, 1) if 2 * j + o >= 0]
                    for o in kts:
                        kt = 2 * j + o
                        tile_of = kt // 4
                        slotk = tile_of % 2
                        col = slotk * 512 + (kt % 4) * 128
                        sb_, sr = next_bank((0, 1))
                        has_b = o != -2
                        mm(sb_[:, :], kwinT[:, gp * 1024 + col:gp * 1024 + col + 128], qg, True, not has_b,
                           [res("kwinT"), qres], [sr])
                        if o in (-4, -3):
                            mm(sb_[:, :], ident, maskW[:, (o + 4) * 512:(o + 5) * 512], False, True, [cres], [sr])
                        elif o in (-1, 0, 1):
                            mm(sb_[:, :], ident, biasS[:, (o + 1) * 2048 + g * 512:(o + 1) * 2048 + (g + 1) * 512], False, True, [cres], [sr])
                        pT = pTs[pti % 3]; pTr = res("pT%d" % (pti % 3)); pti += 1
                        act(pT[:, :], sb_[:, :], AF.Exp, [sr], [pTr])
                        for r in range(4):
                            mm(OW_b[:, r * 65:(r + 1) * 65], pT[:, r * 128:(r + 1) * 128], vwin4[:, slotk * 4 + kt % 4, g, :],
                               first_w, False, [pTr, res("vwin")], [OWr], skip=True)
                            first_w = False
                    stage(6.5)
                    OS_b, OSr = PB[4], PBr[4]
                    first_s = True
                    nkt = 2 * j + 2
                    for c0 in range(0, nkt, 8):
                        nk = min(8, nkt - c0)
                        ri = (c0 // 8) % 2
                        kr, krr = kring[ri], res("kring%d" % ri)
                        vr, vrr = vring[ri], res("vring%d" % ri)
                        dma("sp", kr[:, 0:nk * 128], kslc_d[gp][:, c0 * 128:(c0 + nk) * 128],
                            [res("kslc_d")], [krr], "kring%d" % ri)
                        vr4 = vr.rearrange("p (s x) -> p s x", s=8, x=65)
                        dma("sp", vr[:, 0:nk * 65], vslc_d[g][:, c0 * 65:(c0 + nk) * 65],
                            [res("vslc_d")], [vrr], "vring%d" % ri)
                        for ki in range(nk):
                            kt = c0 + ki
                            o = kt - 2 * j
                            sb_, sr = next_bank((0, 1))
                            mm(sb_[:, :], kr[:, ki * 128:(ki + 1) * 128], qg, True, False, [krr, qres], [sr])
                            a = kt // 16
                            has_b = o in (-1, 0, 1)
                            mm(sb_[:, :], E32[:, (kt % 16) * 128:(kt % 16 + 1) * 128],
                               n4[0:32, a, :, :], False, not has_b, [cres, res("negT4")], [sr])
                            if has_b:
                                mm(sb_[:, :], ident, biasS[:, (o + 1) * 2048 + g * 512:(o + 1) * 2048 + (g + 1) * 512], False, True, [cres], [sr])
                            pT = pTs[pti % 3]; pTr = res("pT%d" % (pti % 3)); pti += 1
                            act(pT[:, :], sb_[:, :], AF.Exp, [sr], [pTr])
                            for r in range(4):
                                mm(OS_b[:, r * 65:(r + 1) * 65], pT[:, r * 128:(r + 1) * 128], vr4[:, ki, :],
                                   first_s, False, [pTr, vrr], [OSr], skip=True)
                                first_s = False
                    stage(6.6)
                    for (ob, obr, br) in ((OW_b, OWr, 2), (OS_b, OSr, 1)):
                        scb = small[:, 272 + br * 8:272 + br * 8 + 8]
                        o3 = ob[:, 0:260].rearrange("p (r x) -> p r x", r=4, x=65)
                        op("dve", lambda e, scb=scb, o3=o3: e.reciprocal(out=scb[:, 0:4], in_=o3[:, :, 64]), reads=[obr], writes=[res("sc")])
                        for r in range(4):
                            h = g * 4 + r
                            v_tt("dve", scb[:, 4 + r:5 + r], scb[:, r:r + 1], gsm[:, h * 3 + br:h * 3 + br + 1], ALU.mult,
                                 [res("sc"), res("gates")], [res("sc")])
                            v_stt("dve", ona[:, h * 64:(h + 1) * 64], o3[:, r, 0:64], scb[:, 4 + r:5 + r], ona[:, h * 64:(h + 1) * 64],
                                  ALU.mult, ALU.add, [obr, res("sc"), onr], [onr])
                if DEBUG and T == NTILES - 1:
                    dma("sp", dbg["onsa"][jl], ona, [onr], [res("dbgd")], "dbg")
                stage(6.7)
                v_copy("dve", onsab[:, :], ona, [onr], [res("onsab")])
                for c in range(8):
                    op("pe", lambda e, c=c: e.transpose(PT[:, c * 128:(c + 1) * 128], onsab[:, c * 128:(c + 1) * 128], ident),
                       reads=[res("onsab"), cres], writes=[PTr])
                oT3 = onsaT.rearrange("p (c m) -> p c m", c=8, m=OW)
                v_copy("dve", oT3[:, :, jl * 128:(jl + 1) * 128], v3(PT[:, 0:1024], 8, 128), [PTr], [res("onsaT")])

            stage(7)
            for c in range(8):
                sl = slice(c * OW, (c + 1) * OW)
                t = tmpF[c % 2][:, 0:OW]; tr = res("tmpF%d" % (c % 2))
                eng = "dve"
                v_tt(eng, t, sg[:, sl], onsaT[:, sl], ALU.mult, [res("sg"), res("onsaT")], [tr])
                t2_ = tmpF[c % 2][:, 256:512]
                v_tt(eng, t2_, sg[:, (8 + c) * OW:(9 + c) * OW], convown[:, sl], ALU.mult, [res("sg"), res("convown")], [tr])
                v_tt(eng, t, t, t2_, ALU.add, [tr], [tr])
                v_tt(eng, t2_, sg[:, (16 + c) * OW:(17 + c) * OW], omemT[:, sl], ALU.mult, [res("sg"), res("omemT")], [tr])
                v_tt(eng, zT[:, sl], t, t2_, ALU.add, [tr], [res("zT")])
            def evac_o(idx, pb, pr):
                v_tt("dve", x1own[:, idx * OW:(idx + 1) * OW], pb[:, 0:OW], x1own[:, idx * OW:(idx + 1) * OW], ALU.add,
                     [pr, res("x1own")], [res("x1own")])
            ws_proj(wo_d, 2, 4, 8, lambda kc: zT[:, kc * OW:(kc + 1) * OW], [res("zT")], OW, evac_o)
            alias([res("sg")], [res("act2")]); alias([res("qT")], [res("h2")]); alias([res("qmT")], [res("sq2")])
            rmsnorm(x1own[:, :], res("x1own"), 16, OW, h2, res("h2"), sq2, res("sq2"))
            ffn(w2in_d, w2out_d, h2, res("h2"), OW, act2, res("act2"), x1own, res("x1own"))
            tok = dma("sp", outT_d.rearrange("(c p) n -> p c n", p=128)[:, :, T * OW:(T + 1) * OW], v3(x1own[:, :], 8, OW),
                      [res("x1own")], [res("outd")], "outst")
            out_tokens.append(tok)

    except StopBuild:
        pass
    finals = {}
    for sm in dsems.values():
        finals[sm] = sm.n
    for k, v in []:
        finals[k] = max(finals.get(k, 0), v)
    if "dbg" in dsems:
        finals[dsems["dbg"]] = dsems["dbg"].n
    S.emit(list(finals.items()))
    return nc


def prep_inputs(inp, core):
    b, p = core // 2, core % 2
    f32 = np.float32
    g = lambda k: np.asarray(inp[k], dtype=f32)[0] if np.asarray(inp[k]).ndim >= 2 and k != "rel_bias" else np.asarray(inp[k], dtype=f32)
    x = np.asarray(inp["x"], dtype=f32); mem = np.asarray(inp["mem"], dtype=f32)
    m = {}
    m["xT"] = np.ascontiguousarray(x[b].T)
    m["memT"] = np.ascontiguousarray(mem[b].T)
    av = np.zeros((128, 2), f32); av[:, 0] = 1.0 if p == 0 else 0.0; av[:, 1] = 1.0 - av[:, 0]
    m["avec"] = av

    def ffn_in(w):
        a = w[:, :DFF].reshape(D, 22, 128); bb = w[:, DFF:].reshape(D, 22, 128)
        return ws_layout(np.stack([a, bb], axis=2).reshape(D, 44 * 128), 4)
    m["w1in"] = ffn_in(g("ffn1_w_in")); m["w1out"] = ws_layout(g("ffn1_w_out"), 1)
    m["w2in"] = ffn_in(g("ffn2_w_in")); m["w2out"] = ws_layout(g("ffn2_w_out"), 1)
    w_in = g("w_in")
    o = 0
    Wq = w_in[:, o:o + 1024]; o += 1024
    Wkv = w_in[:, o:o + 1536]; o += 1536
    Wg = w_in[:, o:o + 48]; o += 48
    Wconv = w_in[:, o:o + 3072]; o += 3072
    Wqm = w_in[:, o:o + 1024]; o += 1024
    Wmg = w_in[:, o:o + 3072]; o += 3072
    kvs = [Wkv[:, i * 256:(i + 1) * 256] for i in range(6)]
    dup = lambda W: np.concatenate([np.concatenate([W[:, gg * 64:(gg + 1) * 64]] * 2, axis=1) for gg in range(4)], axis=1)
    m["wkc"] = np.concatenate([ws_layout(dup(kvs[0]), 4), ws_layout(dup(kvs[1]), 4)], axis=0)
    m["wkn"] = ws_layout(np.concatenate([kvs[2], kvs[4]], axis=1), 4)
    m["wv"] = tm_layout(np.concatenate([kvs[3], kvs[5]], axis=1))[None]
    Wc3 = np.stack([Wconv[:, 0:1024].reshape(D, 8, 128), Wconv[:, 1024:2048].reshape(D, 8, 128),
                    Wconv[:, 2048:3072].reshape(D, 8, 128)], axis=2).reshape(D, 24 * 128)
    m["wconv"] = ws_layout(Wc3, 3)
    cols = []
    for gp in range(2):
        for r in range(4):
            for beta in range(2):
                h = (2 * gp + beta) * 4 + r
                cols.append(Wq[:, h * 64:(h + 1) * 64])
    m["wq"] = ws_layout(np.concatenate(cols, axis=1), 4)
    m["wqm"] = ws_layout(Wqm, 4)
    m["wmg"] = ws_layout(Wmg, 4)
    m["wg"] = tm_layout(Wg)
    m["wo"] = ws_layout(g("w_out"), 4)
    wmkv = g("w_mem_kv")
    m["wmk"] = ws_layout(wmkv[:, :1024], 4)
    m["wmv"] = np.stack([tm_layout(wmkv[:, 1024 + hh * 512:1024 + (hh + 1) * 512]) for hh in range(2)], axis=0)

    def w1stack(w1):
        a = w1.reshape(2, 16, 64, 256).transpose(0, 2, 1, 3).reshape(128, 16 * 256)
        return np.ascontiguousarray(a)
    m["cw1"] = np.stack([w1stack(g("cmp_w1_k")), w1stack(g("cmp_w1_v"))], axis=0)
    w2k = g("cmp_w2_k"); w2v = g("cmp_w2_v")
    cw2 = np.zeros((128, 640), f32)
    for beta in range(2):
        for hc in range(2):
            cw2[:, beta * 256 + hc * 128 + beta * 64: beta * 256 + hc * 128 + beta * 64 + 64] = w2k[hc * 128:(hc + 1) * 128]
    for hc in range(2):
        cw2[:, 512 + hc * 64:512 + (hc + 1) * 64] = w2v[hc * 128:(hc + 1) * 128]
    m["cw2"] = cw2
    pes = lambda pe: pe.reshape(2, 16, 64).transpose(0, 2, 1).reshape(128, 16)
    m["cpe"] = np.ascontiguousarray(np.concatenate([pes(g("cmp_pe_k")), pes(g("cmp_pe_v"))], axis=1))
    gains = np.zeros((128, 64), f32)
    gains[:, 0:8] = col_layout(g("ffn1_norm_g")); gains[:, 8:16] = col_layout(g("mix_norm_g"))
    gains[:, 16:24] = col_layout(g("ffn2_norm_g")); gains[:, 24:32] = col_layout(g("mem_norm_g"))
    gains[:, 32] = np.tile(g("q_norm_g"), 2); gains[:, 33] = np.tile(g("k_norm_g"), 2)
    gains[:, 34:36] = col_layout(g("mem_q_norm_g")); gains[:, 36:38] = col_layout(g("mem_k_norm_g"))
    m["gains"] = gains
    g2 = np.zeros((128, 32), f32)
    cw = g("conv_w")
    for k in range(3):
        g2[:, k * 8:(k + 1) * 8] = col_layout(cw[k])
    g2[:, 24:32] = col_layout(g("conv_b"))
    m["gains2"] = g2
    rb = np.asarray(inp["rel_bias"], dtype=f32)
    mq = np.arange(128); trel = 2 * mq + p
    kk = np.arange(128)
    GS = np.zeros((3, 128, 16, 128), f32); MS = np.zeros((3, 128, 16, 128), f32)
    for oi, o in enumerate((-1, 0, 1)):
        dist = trel[None, :] - 128 * o - kk[:, None]
        bk = rel_bucket_np(dist)
        GS[oi] = rb[bk].transpose(0, 2, 1)
        MS[oi] = np.where(dist >= 0, 0.0, NEGM)[:, None, :]
    m["GS"] = GS.reshape(3, 128, 2048); m["MS"] = MS.reshape(3, 128, 2048)
    m["R31"] = np.ascontiguousarray(np.broadcast_to(rb[31][None, :, None], (128, 16, 128))).reshape(128, 2048).astype(f32)
    MW = np.zeros((2, 128, 4, 128), f32)
    for oi, o in enumerate((-4, -3)):
        dist = trel[None, :] - 128 * o - kk[:, None]
        MW[oi] = np.where(dist < 512, 0.0, NEGM)[:, None, :]
    m["MW"] = MW.reshape(2, 128, 512)
    k64 = np.arange(64)
    GCt = np.zeros((2, 64, 16, 128), f32); MCt = np.zeros((2, 64, 16, 128), f32)
    for jp, cst in enumerate((497, 241)):
        dist = trel[None, :] - 16 * k64[:, None] + cst
        bk = rel_bucket_np(dist)
        GCt[jp] = rb[bk].transpose(0, 2, 1)
        MCt[jp] = np.where(dist >= 0, 0.0, NEGM)[:, None, :]
    m["GC"] = GCt.reshape(2, 64, 2048); m["MC"] = MCt.reshape(2, 64, 2048)
    cq = trel // 64
    xx = np.arange(256) - 128
    at = np.where(xx[None, :] > cq[:, None], -FORCE, np.where(xx[None, :] >= cq[:, None] - 1, FORCE, 0.0)).astype(f32)
    m["addtab"] = at
    cc = np.arange(640); c = cc - 33
    blk = np.arange(128)
    cs = c[:, None] * 16; ss = blk[None, :] * 64
    ov = np.maximum(np.minimum(cs + 32, ss + 64) - np.maximum(cs, ss), 0) / 32.0
    ov = np.where((c[:, None] >= 0) & (c[:, None] <= 510), ov, 0.0).astype(f32)
    m["selmap"] = np.ascontiguousarray(ov.reshape(5, 128, 128).transpose(1, 0, 2)).reshape(128, 640)
    o65 = ((c >= 0) & (c <= 510)).astype(f32).reshape(5, 128).T
    m["ones65"] = np.ascontiguousarray(np.repeat(o65[:, :, None], 4, axis=2)).reshape(128, 20)
    E32 = (np.arange(32)[:, None] == (np.arange(2048)[None, :] // 64)).astype(f32)
    m["E32"] = E32
    Iext = (np.arange(320)[None, :] == (np.arange(64)[:, None] + 128)).astype(f32)
    m["Iext"] = Iext
    cm = np.zeros((128, 384), f32)
    cm[:, 0:128] = np.eye(128)
    cm[:, 128:256] = (np.arange(128)[:, None] // 64 == np.arange(128)[None, :] // 64)
    cm[:, 256:384] = 1.0
    m["cmat"] = cm
    return m


_CACHE = {}


def kernel(**inputs):
    if "nc" not in _CACHE:
        _CACHE["nc"] = build_program()
    nc = _CACHE["nc"]
    in_maps = [prep_inputs(inputs, c) for c in range(8)]
    res = run_bass_kernel_spmd(nc, in_maps, core_ids=list(range(8)))
    x = np.asarray(inputs["x"])
    out = np.zeros(x.shape, dtype=np.float32)
    for c in range(8):
        b, p = c // 2, c % 2
        oT = np.asarray(res.results[c]["outT"])
        out[b, p::2, :] = oT.T
    _CACHE["last"] = res
    return out
```
